# Optimizing a Trainium2 kernel written in Bass

```python
import jax, jax.numpy as jnp
from jax import lax
import numpy as np

D_MODEL = 2048
BATCH = 4
SEQ = 2048
DEPTH = 1
DEC_BATCH = 8
DEC_SEQ = 1
PAST_LEN = 16384
PAGE_SIZE = 128

N_HEADS = 16
HEAD_DIM = D_MODEL // N_HEADS
D_ATTN = N_HEADS * HEAD_DIM
D_CONV = D_MODEL // 2
CONV_W = 3
D_FF = -(-8 * D_MODEL // (3 * 256)) * 256
Q_BLOCK = 128
N_MOD = 6
EPS = 1e-6
SB_BIAS_INIT = -6.0
IN_SIZES = (D_ATTN, D_ATTN, D_ATTN, D_CONV, D_CONV, D_CONV, D_MODEL, D_MODEL)
D_IN = sum(IN_SIZES)
IN_OFFSETS = tuple(int(o) for o in np.cumsum(IN_SIZES)[:-1])

kernel_name = "stickbreak_shortconv_adaln_hybrid_step"


def rmsnorm(x, g):
    xf = x.astype(jnp.float32)
    r = lax.rsqrt(jnp.mean(xf * xf, axis=-1, keepdims=True) + EPS)
    return (xf * r).astype(x.dtype) * g


def stick_breaking(q, k, v, bias, q_idx, k_idx):
    z = (jnp.einsum('bqhd,bkhd->bhqk', q, k).astype(jnp.float32) * (HEAD_DIM ** -0.5)
         + bias.astype(jnp.float32)[None, :, None, None])
    causal = k_idx[None, :] < q_idx[:, None]
    log_keep = jnp.where(causal, jax.nn.log_sigmoid(-z), 0.0)
    suffix = lax.cumsum(log_keep, axis=3, reverse=True) - log_keep
    w = jnp.where(causal, jnp.exp(jax.nn.log_sigmoid(z) + suffix), 0.0)
    return jnp.einsum('bhqk,bkhd->bqhd', w.astype(v.dtype), v)


def prompt_attend(q, k, v, bias):
    b, s, h, d = q.shape
    nb = s // Q_BLOCK
    qb = q.reshape(b, nb, Q_BLOCK, h, d).transpose(1, 0, 2, 3, 4)
    k_idx = jnp.arange(s)

    def one_block(args):
        qi, i = args
        q_idx = i * Q_BLOCK + jnp.arange(Q_BLOCK)
        return stick_breaking(qi, k, v, bias, q_idx, k_idx)

    out = lax.map(one_block, (qb, jnp.arange(nb)))
    return out.transpose(1, 0, 2, 3, 4).reshape(b, s, h, d)


def make_sample_attend(past_k, past_v):
    def attend(q, k, v, bias):
        p = past_k.shape[1]
        t = q.shape[1]
        kk = jnp.concatenate([past_k.astype(k.dtype), k], axis=1)
        vv = jnp.concatenate([past_v.astype(v.dtype), v], axis=1)
        return stick_breaking(q, kk, vv, bias, p + jnp.arange(t), jnp.arange(p + t))
    return attend


def mixer(h, conv_prefix, attend, w_in, b_sb, conv_w, w_attn_out, w_conv_out, w_out):
    b, l, _ = h.shape
    q, k, v, cb, cc, ch, ga, gb = jnp.split(h @ w_in, IN_OFFSETS, axis=-1)
    q = q.reshape(b, l, N_HEADS, HEAD_DIM)
    k = k.reshape(b, l, N_HEADS, HEAD_DIM)
    v = v.reshape(b, l, N_HEADS, HEAD_DIM)
    o_attn = attend(q, k, v, b_sb).reshape(b, l, D_ATTN)
    u = cc * ch
    u_full = jnp.concatenate([conv_prefix.astype(u.dtype), u], axis=1)
    conv = sum(conv_w[j] * u_full[:, j:j + l] for j in range(CONV_W))
    o_conv = cb * conv
    merged = jax.nn.sigmoid(ga) * (o_attn @ w_attn_out) + jax.nn.sigmoid(gb) * (o_conv @ w_conv_out)
    return merged @ w_out, k, v, u_full[:, -(CONV_W - 1):]


def decoder_layer(x, c, conv_prefix, attend, g_mix, g_ffn, w_mod, b_mod, w_in, b_sb, conv_w,
                  w_attn_out, w_conv_out, w_out, w_ffn_in, w_ffn_out):
    mod = (jax.nn.silu(c) @ w_mod + b_mod)[:, None, :]
    shift1, scale1, gate1, shift2, scale2, gate2 = jnp.split(mod, N_MOD, axis=-1)
    h = rmsnorm(x, g_mix) * (1 + scale1) + shift1
    mix_out, k, v, conv_state = mixer(h, conv_prefix, attend, w_in, b_sb, conv_w,
                                      w_attn_out, w_conv_out, w_out)
    x = x + gate1 * mix_out
    h = rmsnorm(x, g_ffn) * (1 + scale2) + shift2
    a, bb = jnp.split(h @ w_ffn_in, 2, axis=-1)
    x = x + gate2 * ((jax.nn.silu(a) * bb) @ w_ffn_out)
    return x, k, v, conv_state


def setup_inputs(seed: int = 0) -> dict:
    key = jax.random.key(seed)
    ks = jax.random.split(key, 24)
    n_pages = PAST_LEN // PAGE_SIZE
    n_used = DEC_BATCH * n_pages
    n_pool = n_used + max(1, n_used // 4)
    f32 = jnp.float32

    def nrm(k, shape, scale):
        return jax.random.normal(k, shape, f32) * scale

    perm = jax.random.permutation(ks[0], n_pool)[:n_used]
    page_table = perm.reshape(DEC_BATCH, n_pages).astype(jnp.int32)
    return {
        "x_prompt": nrm(ks[1], (BATCH, SEQ, D_MODEL), 1.0),
        "x_sample": nrm(ks[2], (DEC_BATCH, DEC_SEQ, D_MODEL), 1.0),
        "cache_k": nrm(ks[3], (DEPTH, n_pool, PAGE_SIZE, N_HEADS, HEAD_DIM), 1.0),
        "cache_v": nrm(ks[4], (DEPTH, n_pool, PAGE_SIZE, N_HEADS, HEAD_DIM), 1.0),
        "state_conv": nrm(ks[5], (DEPTH, DEC_BATCH, CONV_W - 1, D_CONV), 1.0),
        "page_table": page_table,
        "c_prompt": nrm(ks[6], (BATCH, D_MODEL), 1.0),
        "c_sample": nrm(ks[7], (DEC_BATCH, D_MODEL), 1.0),
        "g_mix": 1.0 + nrm(ks[8], (DEPTH, D_MODEL), 0.01),
        "g_ffn": 1.0 + nrm(ks[9], (DEPTH, D_MODEL), 0.01),
        "g_final": 1.0 + nrm(ks[10], (D_MODEL,), 0.01),
        "w_mod": nrm(ks[11], (DEPTH, D_MODEL, N_MOD * D_MODEL), 0.5 * D_MODEL ** -0.5),
        "b_mod": nrm(ks[12], (DEPTH, N_MOD * D_MODEL), 0.01),
        "w_in": nrm(ks[13], (DEPTH, D_MODEL, D_IN), D_MODEL ** -0.5),
        "b_sb": SB_BIAS_INIT + nrm(ks[20], (DEPTH, N_HEADS), 0.1),
        "conv_w": nrm(ks[14], (DEPTH, CONV_W, D_CONV), CONV_W ** -0.5),
        "w_attn_out": nrm(ks[15], (DEPTH, D_ATTN, D_MODEL), D_ATTN ** -0.5),
        "w_conv_out": nrm(ks[16], (DEPTH, D_CONV, D_MODEL), D_CONV ** -0.5),
        "w_out": nrm(ks[17], (DEPTH, D_MODEL, D_MODEL), D_MODEL ** -0.5),
        "w_ffn_in": nrm(ks[18], (DEPTH, D_MODEL, 2 * D_FF), D_MODEL ** -0.5),
        "w_ffn_out": nrm(ks[19], (DEPTH, D_FF, D_MODEL), D_FF ** -0.5),
    }


def reference(x_prompt, x_sample, cache_k, cache_v, state_conv, page_table, c_prompt, c_sample,
              g_mix, g_ffn, g_final, w_mod, b_mod, w_in, b_sb, conv_w, w_attn_out, w_conv_out,
              w_out, w_ffn_in, w_ffn_out):
    xp, xs = x_prompt, x_sample
    db = page_table.shape[0]
    k_p, v_p, cs_p, k_s, v_s, cs_s = [], [], [], [], [], []
    for l in range(DEPTH):
        lw = (g_mix[l], g_ffn[l], w_mod[l], b_mod[l], w_in[l], b_sb[l], conv_w[l],
              w_attn_out[l], w_conv_out[l], w_out[l], w_ffn_in[l], w_ffn_out[l])
        prefix0 = jnp.zeros((xp.shape[0], CONV_W - 1, D_CONV), xp.dtype)
        xp, kp, vp, csp = decoder_layer(xp, c_prompt, prefix0, prompt_attend, *lw)
        past_k = cache_k[l][page_table].reshape(db, -1, N_HEADS, HEAD_DIM)
        past_v = cache_v[l][page_table].reshape(db, -1, N_HEADS, HEAD_DIM)
        xs, ks_, vs_, css = decoder_layer(xs, c_sample, state_conv[l],
                                          make_sample_attend(past_k, past_v), *lw)
        k_p.append(kp); v_p.append(vp); cs_p.append(csp)
        k_s.append(ks_); v_s.append(vs_); cs_s.append(css)
    y_prompt = rmsnorm(xp, g_final)
    y_sample = rmsnorm(xs, g_final)
    return (y_prompt, y_sample, jnp.stack(k_p), jnp.stack(v_p), jnp.stack(cs_p),
            jnp.stack(k_s), jnp.stack(v_s), jnp.stack(cs_s))
```

```python
import numpy as np
from contextlib import ExitStack
import concourse.bass as bass
import concourse.mybir as mybir
from concourse.bass_utils import run_bass_kernel_spmd

F32 = mybir.dt.float32
BF16 = mybir.dt.bfloat16
I32 = mybir.dt.int32
U8 = mybir.dt.uint8
AF = mybir.ActivationFunctionType
ALU = mybir.AluOpType
AX = mybir.AxisListType
DT_SIZE = {F32: 4, BF16: 2, I32: 4, U8: 1}

D = 2048
KC = 16
H = 16
NT = 1024
G = 512
DC = 1024
DFF = 5632
D_IN = 13312
OFF_Q, OFF_K, OFF_V, OFF_CB, OFF_CC, OFF_CH, OFF_GA, OFF_GB = 0, 2048, 4096, 6144, 7168, 8192, 9216, 11264
NPAGES = 128
NPOOL = 1280
EPS = 1e-6
NEG = -30000.0
QSCALE = 128 ** -0.5


class Tile:
    __slots__ = ("name", "writers", "readers", "excl")

    def __init__(self, name, inherited=(), excl=False):
        self.name = name
        self.writers = list(inherited)
        self.readers = []
        self.excl = excl


class Op:
    __slots__ = ("eng", "fn", "deps", "kind", "semkey", "val", "need_signal", "ndma", "name", "seq")
    _ctr = [0]

    def __init__(self, eng, fn, kind, name=""):
        self.eng = eng
        self.fn = fn
        self.kind = kind
        self.deps = []
        self.semkey = None
        self.val = None
        self.need_signal = False
        self.ndma = 0
        self.name = name
        Op._ctr[0] += 1
        self.seq = Op._ctr[0]


def _compress(ops):
    last = {}
    out = {}
    for o in ops:
        if o.kind == "dma":
            out[id(o)] = o
        elif o.eng not in last or last[o.eng].seq < o.seq:
            last[o.eng] = o
    return list(out.values()) + list(last.values())


class Buf:
    def __init__(self, name, ap, start, end, inherited):
        self.name = name
        self.ap = ap
        self.start = start
        self.end = end
        self.inherited = inherited
        self.tiles = {}

    def t(self, key=0):
        tl = self.tiles.get(key)
        if tl is None:
            tl = Tile(f"{self.name}[{key}]", self.inherited)
            self.tiles[key] = tl
        return tl

    def hazards(self):
        ops = list(self.inherited)
        for tl in self.tiles.values():
            ops += tl.writers + tl.readers
        return _compress(ops)


class Arena:
    def __init__(self, base_ap, size):
        self.base = base_ap
        self.size = size
        self.live = []
        self.dead = []
        self.peak = 0

    BASE = 46080

    def alloc(self, name, free_shape, dtype, align=64, top=False, at=None):
        nbytes = int(np.prod(free_shape)) * DT_SIZE[dtype]
        self.live.sort(key=lambda b: b.start)
        if at is not None:
            start = self.BASE + at * 1024
            for b in self.live:
                assert b.end <= start or b.start >= start + nbytes, (name, b.name, b.start, b.end, start, nbytes)
            assert start + nbytes <= self.size, name
        else:
            gaps = []
            pos = 0
            for b in self.live:
                if b.start - pos >= nbytes:
                    gaps.append((pos, b.start))
                pos = max(pos, (b.end + align - 1) // align * align)
            if self.size - pos >= nbytes:
                gaps.append((pos, self.size))
            if not gaps:
                raise RuntimeError(f"SBUF arena full allocating {name} ({nbytes}B); live="
                                   + ",".join(f"{b.name}:{b.start}-{b.end}" for b in self.live))
            gs, ge = gaps[0]
            start = gs
        end = start + nbytes
        inh = []
        nd = []
        for (s, e, ops) in self.dead:
            if s < end and start < e:
                inh += ops
                if s >= start and e <= end:
                    continue
            nd.append((s, e, ops))
        self.dead = nd
        inh = _compress(inh)
        ap = self.base[:, start:end]
        if dtype != U8:
            ap = ap.bitcast(dtype)
        if len(free_shape) == 2:
            ap = ap.rearrange("p (a b) -> p a b", a=free_shape[0])
        elif len(free_shape) == 3:
            ap = ap.rearrange("p (a b c) -> p a b c", a=free_shape[0], b=free_shape[1])
        b = Buf(name, ap, start, end, inh)
        self.live.append(b)
        self.peak = max(self.peak, end)
        return b

    def free(self, buf):
        self.live.remove(buf)
        self.dead.append((buf.start, buf.end, buf.hazards()))


class Prog:
    ENGS = ("pe", "act", "dve", "pool", "sp")

    def __init__(self, nc):
        self.nc = nc
        self.ops = []
        self.dma_counts = {}
        self.final_waits = []

    def _add(self, op, reads, writes):
        deps = {}
        for t in reads:
            for w in t.writers:
                deps[id(w)] = (w, True)
            if t.excl:
                for r in t.readers:
                    if r.eng != op.eng and id(r) not in deps:
                        deps[id(r)] = (r, False)
        for t in writes:
            for w in t.writers:
                if id(w) not in deps:
                    deps[id(w)] = (w, False)
            for r in t.readers:
                if id(r) not in deps:
                    deps[id(r)] = (r, False)
        op.deps = [d for d in deps.values() if d[0] is not op]
        for t in reads:
            t.readers.append(op)
            if len(t.readers) > 24:
                t.readers = _compress(t.readers)
        for t in writes:
            t.writers = [op]
            t.readers = []
        self.ops.append(op)
        return op

    def add(self, eng, fn, reads=(), writes=(), name=""):
        return self._add(Op(eng, fn, "c", name), reads, writes)

    def dma(self, eng, fn, semkey, ndma, reads=(), writes=(), name="", final=False):
        op = Op(eng, fn, "dma", name)
        op.semkey = semkey
        op.ndma = ndma
        c = self.dma_counts.get(semkey, 0) + ndma
        self.dma_counts[semkey] = c
        op.val = 16 * c
        self._add(op, reads, writes)
        if final:
            self.final_waits.append(op)
        return op

    @staticmethod
    def _needs_sync(cons, prod, is_raw):
        if prod.kind == "dma" or cons.kind == "dma":
            return True
        if prod.eng != cons.eng:
            return True
        if cons.eng == "pe":
            return False
        return True

    def emit(self, stack):
        nc = self.nc
        for op in self.ops:
            for (d, is_raw) in op.deps:
                if self._needs_sync(op, d, is_raw):
                    d.need_signal = True
        counters = {e: 0 for e in self.ENGS}
        for op in self.ops:
            if op.kind == "c" and op.need_signal:
                counters[op.eng] += 1
                op.val = counters[op.eng]
                op.semkey = "eng_" + op.eng
        sems = {}
        for e in self.ENGS:
            sems["eng_" + e] = stack.enter_context(nc.semaphore("s_eng_" + e))
        for k in self.dma_counts:
            sems[k] = stack.enter_context(nc.semaphore("s_" + k))
        per_eng = {e: [] for e in self.ENGS}
        for op in self.ops:
            per_eng[op.eng].append(op)
        block = stack.enter_context(nc.Block())

        def make(ename, ops):
            def body(eng):
                waited = {}
                for op in ops:
                    for (d, is_raw) in op.deps:
                        if not self._needs_sync(op, d, is_raw):
                            continue
                        if waited.get(d.semkey, 0) >= d.val:
                            continue
                        eng.wait_ge(sems[d.semkey], d.val)
                        waited[d.semkey] = d.val
                    if op.kind == "c":
                        ins = op.fn(eng)
                        if op.need_signal:
                            ins.then_inc(sems[op.semkey], 1)
                    else:
                        lst = op.fn(eng)
                        assert len(lst) == op.ndma, (op.name, len(lst), op.ndma)
                        for ins in lst:
                            ins.then_inc(sems[op.semkey], 16)
                if ename == "sp":
                    for op in self.final_waits:
                        if waited.get(op.semkey, 0) >= op.val:
                            continue
                        eng.wait_ge(sems[op.semkey], op.val)
                        waited[op.semkey] = op.val
            return body

        block.tensor(make("pe", per_eng["pe"]))
        block.scalar(make("act", per_eng["act"]))
        block.vector(make("dve", per_eng["dve"]))
        block.gpsimd(make("pool", per_eng["pool"]))
        block.sync(make("sp", per_eng["sp"]))
        return counters


class WQ:
    NS = 2
    SLOT_ELEMS = 8192

    def __init__(self, P, ar, plan):
        self.P = P
        self.plan = plan
        self.rec = []
        self.i = 0
        self.issued = 0
        self.done_n = 0
        self.slots = [ar.alloc(f"wslot{s}", [self.SLOT_ELEMS], BF16) for s in range(self.NS)]

    def _view(self, s, nk, ncols):
        return self.slots[s].ap[:, 0:nk * ncols].rearrange("p (k n) -> p k n", k=nk)

    def _issue(self, j, desc):
        (w, r0, nr, c0, ncols) = desc
        s = j % self.NS
        nk = nr // 128
        view = self._view(s, nk, ncols)
        src = w[r0:r0 + nr, c0:c0 + ncols].rearrange("(k p) n -> p k n", p=128)
        self.P.dma("pool", lambda e, view=view, src=src: [e.dma_start(out=view, in_=src)],
                   f"wslot{s}", 1, writes=[self.slots[s].t()], name=f"wload{j}")

    def _pump(self):
        if self.plan is None:
            return
        lim = min(self.done_n + self.NS, len(self.plan))
        while self.issued < lim:
            self._issue(self.issued, self.plan[self.issued])
            self.issued += 1

    def next(self, w, r0, nr, c0, ncols):
        desc = (w, r0, nr, c0, ncols)
        idx = self.i
        self.i += 1
        self.rec.append(desc)
        assert self.done_n >= idx - self.NS + 1, "too many live weight tiles"
        if self.plan is None:
            self._issue(idx, desc)
        else:
            pd = self.plan[idx]
            assert pd[1:] == desc[1:], (idx, pd[1:], desc[1:])
            self._pump()
            assert self.issued > idx
        s = idx % self.NS
        return self._view(s, nr // 128, ncols), self.slots[s].t()

    def done(self):
        self.done_n += 1
        self._pump()


def build_core_program(nc, T, plan, debug=None):
    st = ExitStack()
    with st:
        SZ = 206 * 1024
        arena_t = st.enter_context(nc.sbuf_tensor("arena", [128, SZ], U8))
        ar = Arena(arena_t[:, :], SZ)
        banks = [st.enter_context(nc.psum_tensor(f"bank{i}", [128, 512], F32)) for i in range(8)]
        bank_t = [Tile(f"bank{i}", excl=True) for i in range(8)]
        P = Prog(nc)
        wq = WQ(P, ar, plan)
        NCD = dict(allow_slow_non_contiguous=True)

        class RR:
            def __init__(self, ids):
                self.ids = ids
                self.i = 0

            def next(self):
                b = self.ids[self.i % len(self.ids)]
                self.i += 1
                return b
        acc_rr = RR([0, 1, 2, 3, 4, 5])
        SB = 7

        def bk(i):
            return banks[i][:, :]

        ones_f = ar.alloc("ones_f", [128], F32)
        zeros_b = ar.alloc("zeros_b", [512], BF16)
        ones_b = ar.alloc("ones_b", [128], BF16)
        ident_f = ar.alloc("ident_f", [128], F32)
        ident_b = ar.alloc("ident_b", [128], BF16)
        tri_b = ar.alloc("tri_b", [128], BF16)
        tri_f = ar.alloc("tri_f", [128], F32)
        masks = ar.alloc("masks", [4, 512], BF16)
        flags = ar.alloc("flags", [2], F32)
        bsb = ar.alloc("bsb", [16], F32)
        bsb_o = ar.alloc("bsb_o", [16], F32)
        gm2 = ar.alloc("gm2", [16, 2], F32)
        gf2 = ar.alloc("gf2", [16, 2], F32)
        gfin = ar.alloc("gfin", [16], F32)
        bm2 = ar.alloc("bm2", [96, 2], F32)
        cw = ar.alloc("cw", [8, 3], F32)
        stc = ar.alloc("stc", [8, 2], F32)
        modT = ar.alloc("modT", [96, 2], F32)
        A1 = ar.alloc("A1", [16, 2], F32)
        A2 = ar.alloc("A2", [16, 2], F32)
        cT = ar.alloc("cT", [16, 2], F32)
        scT = ar.alloc("scT", [16, 2], BF16)
        xs = ar.alloc("xs", [16], F32)
        hs = ar.alloc("hs", [16], BF16)
        qs = ar.alloc("qs", [16], F32)
        ks = ar.alloc("ks", [16], F32)
        vs = ar.alloc("vs", [16], F32)
        os_b = ar.alloc("os_b", [16], BF16)
        ocs_b = ar.alloc("ocs_b", [8], BF16)
        mgs_b = ar.alloc("mgs_b", [16], BF16)
        gs_b = ar.alloc("gs_b", [44], BF16)
        tiny = ar.alloc("tiny", [64], F32)
        ptb = ar.alloc("ptb", [128], I32)
        pidx = ar.alloc("pidx", [128], I32)
        iof = ar.alloc("iof", [1], F32)
        cpst = ar.alloc("cpst", [8, 2], F32)
        csst = ar.alloc("csst", [8, 2], F32)
        sm = ar.alloc("sm", [16], F32)
        smc = ar.alloc("smc", [4, 16], F32)

        P.add("pool", lambda e: e.memset(ones_f.ap, 1.0), writes=[ones_f.t()])
        P.add("pool", lambda e: e.memset(ones_b.ap, 1.0), writes=[ones_b.t()])
        P.add("pool", lambda e: e.memset(zeros_b.ap, 0.0), writes=[zeros_b.t()])
        P.add("pool", lambda e: e.affine_select(out=ident_f.ap, in_=ones_f.ap, pattern=[[-1, 128]],
                                                compare_op=ALU.is_equal, fill=0.0, base=0, channel_multiplier=1),
              reads=[ones_f.t()], writes=[ident_f.t()])
        P.add("pool", lambda e: e.affine_select(out=ident_b.ap, in_=ones_b.ap, pattern=[[-1, 128]],
                                                compare_op=ALU.is_equal, fill=0.0, base=0, channel_multiplier=1),
              reads=[ones_b.t()], writes=[ident_b.t()])
        P.add("pool", lambda e: e.affine_select(out=tri_b.ap, in_=ones_b.ap, pattern=[[-1, 128]],
                                                compare_op=ALU.is_ge, fill=0.0, base=0, channel_multiplier=1),
              reads=[ones_b.t()], writes=[tri_b.t()])
        P.add("pool", lambda e: e.affine_select(out=tri_f.ap, in_=ones_f.ap, pattern=[[-1, 128]],
                                                compare_op=ALU.is_ge, fill=0.0, base=0, channel_multiplier=1),
              reads=[ones_f.t()], writes=[tri_f.t()])
        for m in range(4):
            P.add("pool", lambda e, m=m: e.affine_select(out=masks.ap[:, m, :], in_=zeros_b.ap, pattern=[[1, 512]],
                                                         compare_op=ALU.is_gt, fill=NEG, base=-128 * m,
                                                         channel_multiplier=-1),
                  reads=[zeros_b.t()], writes=[masks.t(m)])

        def fm_vec(ap1d):
            return ap1d.rearrange("(c p) -> p c", p=128)

        def ld_consts(e):
            l = []
            l.append(e.dma_start(out=flags.ap, in_=T["flags"]))
            l.append(e.dma_start(out=bsb.ap, in_=T["b_sb"].partition_broadcast(128)))
            l.append(e.dma_start(out=gm2.ap[:, :, 0], in_=fm_vec(T["g_mix"]), **NCD))
            l.append(e.dma_start(out=gm2.ap[:, :, 1], in_=fm_vec(T["g_mix"]), **NCD))
            l.append(e.dma_start(out=gf2.ap[:, :, 0], in_=fm_vec(T["g_ffn"]), **NCD))
            l.append(e.dma_start(out=gf2.ap[:, :, 1], in_=fm_vec(T["g_ffn"]), **NCD))
            l.append(e.dma_start(out=gfin.ap, in_=fm_vec(T["g_final"]), **NCD))
            l.append(e.dma_start(out=bm2.ap[:, :, 0], in_=fm_vec(T["b_mod"]), **NCD))
            l.append(e.dma_start(out=bm2.ap[:, :, 1], in_=fm_vec(T["b_mod"]), **NCD))
            for jt in range(3):
                l.append(e.dma_start(out=cw.ap[:, :, jt], in_=fm_vec(T["conv_w"][jt, :]), **NCD))
            for jt in range(2):
                l.append(e.dma_start(out=stc.ap[:, :, jt], in_=fm_vec(T["state_conv"][jt, :]), **NCD))
            l.append(e.dma_start(out=cT.ap[:, :, 0], in_=fm_vec(T["c_p"]), **NCD))
            l.append(e.dma_start(out=cT.ap[:, :, 1], in_=fm_vec(T["c_s"]), **NCD))
            l.append(e.dma_start(out=xs.ap, in_=fm_vec(T["x_s"]), **NCD))
            l.append(e.dma_start(out=ptb.ap, in_=T["page_tbl"].partition_broadcast(128)))
            return l
        cts = [flags.t(), bsb.t(), gm2.t(), gf2.t(), gfin.t(), bm2.t(), cw.t(), stc.t(), cT.t(), xs.t(), ptb.t()]
        P.dma("sp", ld_consts, "consts", 18, writes=cts, name="ld_consts")
        P.add("pool", lambda e: e.iota(iof.ap, [[0, 1]], base=0, channel_multiplier=1, allow_small_or_imprecise_dtypes=True),
              writes=[iof.t()])
        P.add("dve", lambda e: e.tensor_scalar(out=pidx.ap, in0=ptb.ap, scalar1=128.0, scalar2=iof.ap[:, 0:1], op0=ALU.mult, op1=ALU.add),
              reads=[ptb.t(), iof.t()], writes=[pidx.t()])
        P.add("dve", lambda e: e.tensor_scalar(out=bsb_o.ap, in0=bsb.ap, scalar1=flags.ap[:, 1:2], scalar2=None,
                                               op0=ALU.add),
              reads=[bsb.t(), flags.t()], writes=[bsb_o.t()])
        P.add("act", lambda e: e.activation(out=scT.ap, in_=cT.ap, func=AF.Silu), reads=[cT.t()], writes=[scT.t()])

        def fm_group(wv, wt, col, nk, movs, outs, name=""):
            reads = [wt]
            for (_, tl) in movs:
                reads += tl
            writes = [bank_t[b] for (_, b) in outs]

            def fn(e):
                ins = None
                for k in range(nk):
                    lhsT = wv[:, k, col * 128:(col + 1) * 128]
                    for (apf, _), (oap, _) in zip(movs, outs):
                        ins = e.matmul(oap, lhsT=lhsT, rhs=apf(k), start=(k == 0), stop=(k == nk - 1))
                return ins
            P.add("pe", fn, reads=reads, writes=writes, name=name)

        mov_c = (lambda k: scT.ap[:, k, :], [scT.t()])
        for tix in range(24):
            wv, wt = wq.next(T["w_mod"], 0, 2048, tix * 512, 512)
            for j in range(4):
                fm_group(wv, wt, j, 16, [mov_c], [(banks[SB][:, 2 * j:2 * j + 2], SB)], name="mod")
            P.add("dve", lambda e, tix=tix: e.tensor_tensor(
                out=modT.ap[:, tix * 4:tix * 4 + 4, :],
                in0=banks[SB][:, 0:8].rearrange("p (a b) -> p a b", b=2),
                in1=bm2.ap[:, tix * 4:tix * 4 + 4, :], op=ALU.add),
                reads=[bank_t[SB], bm2.t()], writes=[modT.t(tix)])
            wq.done()
        mod_all = [modT.t(i) for i in range(24)]
        P.add("dve", lambda e: e.scalar_tensor_tensor(out=A1.ap, in0=modT.ap[:, 16:32, :], scalar=1.0, in1=gm2.ap,
                                                      op0=ALU.add, op1=ALU.mult),
              reads=mod_all + [gm2.t()], writes=[A1.t()])
        P.add("dve", lambda e: e.scalar_tensor_tensor(out=A2.ap, in0=modT.ap[:, 64:80, :], scalar=1.0, in1=gf2.ap,
                                                      op0=ALU.add, op1=ALU.mult),
              reads=mod_all + [gf2.t()], writes=[A2.t()])
        MODT = Tile("modT_done")
        P.add("dve", lambda e: e.tensor_copy(out=tiny.ap[:, 0:1], in_=modT.ap[:, 0, 0:1]), reads=mod_all + [A1.t(), A2.t()],
              writes=[MODT, tiny.t()])

        def B1(c, t):
            return modT.ap[:, c, t:t + 1]

        def GATE1(c, t):
            return modT.ap[:, 32 + c, t:t + 1]

        def B2(c, t):
            return modT.ap[:, 48 + c, t:t + 1]

        def GATE2(c, t):
            return modT.ap[:, 80 + c, t:t + 1]

        def sample_norm(Aap, Bap, out_buf, name):
            sq = tiny.ap[:, 0:16]
            ssum = tiny.ap[:, 16:17]
            rs = tiny.ap[:, 17:18]
            tmp = tiny.ap[:, 32:48]
            P.add("dve", lambda e: e.tensor_tensor(out=sq, in0=xs.ap, in1=xs.ap, op=ALU.mult),
                  reads=[xs.t()], writes=[tiny.t()])
            P.add("dve", lambda e: e.tensor_reduce(out=ssum, in_=sq, axis=AX.X, op=ALU.add),
                  reads=[tiny.t()], writes=[tiny.t()])
            P.add("pe", lambda e: e.matmul(banks[SB][:, 0:1], lhsT=ones_f.ap, rhs=ssum, start=True, stop=True),
                  reads=[tiny.t(), ones_f.t()], writes=[bank_t[SB]])
            P.add("dve", lambda e: e.tensor_scalar(out=rs, in0=banks[SB][:, 0:1], scalar1=1.0 / D, scalar2=EPS,
                                                   op0=ALU.mult, op1=ALU.add),
                  reads=[bank_t[SB]], writes=[tiny.t()])
            P.add("act", lambda e: e.sqrt(out=rs, in_=rs), reads=[tiny.t()], writes=[tiny.t()])
            P.add("dve", lambda e: e.reciprocal(out=rs, in_=rs), reads=[tiny.t()], writes=[tiny.t()])
            P.add("dve", lambda e: e.scalar_tensor_tensor(out=tmp, in0=xs.ap, scalar=rs, in1=Aap,
                                                          op0=ALU.mult, op1=ALU.mult),
                  reads=[tiny.t(), xs.t(), MODT], writes=[tiny.t()])
            if Bap is not None:
                P.add("dve", lambda e: e.tensor_tensor(out=out_buf.ap, in0=tmp, in1=Bap, op=ALU.add),
                      reads=[tiny.t(), MODT], writes=[out_buf.t()])
            else:
                P.add("dve", lambda e: e.tensor_copy(out=out_buf.ap, in_=tmp), reads=[tiny.t()], writes=[out_buf.t()])

        STATS = 6

        def group_norm_stats(src_fn, src_tiles_fn, sqb, rstd, tag):
            for k in range(KC):
                sb_ = sqb[k % 2]
                P.add("act", lambda e, k=k, sb_=sb_: e.activation(out=sb_.ap, in_=src_fn(k), func=AF.Square),
                      reads=src_tiles_fn(k), writes=[sb_.t()])
                P.add("pe", lambda e, k=k, sb_=sb_: e.matmul(bk(STATS), lhsT=ones_b.ap, rhs=sb_.ap,
                                                             start=(k == 0), stop=(k == KC - 1)),
                      reads=[sb_.t(), ones_b.t()], writes=[bank_t[STATS]])
            P.add("dve", lambda e: e.tensor_scalar(out=rstd.ap, in0=bk(STATS), scalar1=1.0 / D, scalar2=EPS,
                                                   op0=ALU.mult, op1=ALU.add),
                  reads=[bank_t[STATS]], writes=[rstd.t()])
            P.add("act", lambda e: e.sqrt(out=rstd.ap, in_=rstd.ap), reads=[rstd.t()], writes=[rstd.t()])
            P.add("dve", lambda e: e.reciprocal(out=rstd.ap, in_=rstd.ap), reads=[rstd.t()], writes=[rstd.t()])

        hT = ar.alloc("hT", [16, 1024], BF16, at=129)
        hTo = ar.alloc("hTo", [16, 1024], BF16, at=97)
        xin = [ar.alloc("xin0", [4, 2048], F32)]
        xTg = ar.alloc("xTg", [16, 512], F32)
        sqb = [ar.alloc(f"sqb{i}", [512], BF16) for i in range(2)]
        rstd = ar.alloc("rstd", [512], F32)
        ntmp = [ar.alloc(f"ntmp{i}", [512], F32) for i in range(2)]

        def load_x(src, g, xb):
            srcv = src[g * 512:(g + 1) * 512, :].rearrange("(b p) f -> p b f", p=128)
            P.dma("sp", lambda e: [e.dma_start(out=xb.ap, in_=srcv)], "xin0", 1,
                  writes=[xb.t()], name="load_x")

        def transpose_group(xb, dst_fn, dst_tile_fn):
            for k in range(KC):
                b = acc_rr.next()

                def tr(e, k=k, b=b):
                    ins = None
                    for j in range(4):
                        ins = e.transpose(out=banks[b][:, j * 128:(j + 1) * 128],
                                          in_=xb.ap[:, j, k * 128:(k + 1) * 128], identity=ident_f.ap)
                    return ins
                P.add("pe", tr, reads=[xb.t(), ident_f.t()], writes=[bank_t[b]], name="xtr")
                P.add("dve", lambda e, k=k, b=b: e.tensor_copy(out=dst_fn(k), in_=bk(b)),
                      reads=[bank_t[b]], writes=[dst_tile_fn(k)])

        def norm_apply(src_fn, src_tile_fn, dst_buf, g, Aap_fn, Bap_fn, rstd, ntmp):
            for k in range(KC):
                tb_ = ntmp[k % 2]
                P.add("dve", lambda e, k=k, tb_=tb_: e.tensor_tensor(out=tb_.ap, in0=src_fn(k), in1=rstd.ap, op=ALU.mult),
                      reads=[src_tile_fn(k), rstd.t()], writes=[tb_.t()])
                P.add("act", lambda e, k=k, tb_=tb_: e.activation(out=dst_buf.ap[:, k, g * 512:(g + 1) * 512], in_=tb_.ap,
                                                                  func=AF.Identity, bias=Bap_fn(k), scale=Aap_fn(k)),
                      reads=[tb_.t(), MODT], writes=[dst_buf.t((g, k))])

        groups = [(T["x_own"], 0, hT), (T["x_own"], 1, hT), (T["x_oth"], 0, hTo), (T["x_oth"], 1, hTo)]
        for gi, (src, g, dst) in enumerate(groups):
            load_x(src, g, xin[0])
            transpose_group(xin[0], lambda k: xTg.ap[:, k, :], lambda k: xTg.t(k))
            group_norm_stats(lambda k: xTg.ap[:, k, :], lambda k: [xTg.t(k)], sqb, rstd, f"n1_{gi}")
            norm_apply(lambda k: xTg.ap[:, k, :], lambda k: xTg.t(k), dst, g,
                       lambda k: A1.ap[:, k, 0:1], lambda k: B1(k, 0), rstd, ntmp)
        sample_norm(A1.ap[:, :, 1], modT.ap[:, 0:16, 1], hs, "sn1")
        ar.free(xTg)
        for b_ in xin + sqb + ntmp + [rstd]:
            ar.free(b_)

        mov_own = [(lambda k, g=g: hT.ap[:, k, g * 512:(g + 1) * 512], [hT.t((g, k)) for k in range(KC)]) for g in range(2)]
        mov_oth = [(lambda k, g=g: hTo.ap[:, k, g * 512:(g + 1) * 512], [hTo.t((g, k)) for k in range(KC)]) for g in range(2)]
        mov_hs = (lambda k: hs.ap[:, k:k + 1], [hs.t()])

        HP = 2
        NHG = H // HP
        WC = HP * 128
        oT = ar.alloc("oT", [16, 1024], BF16, at=65)
        qT2 = [ar.alloc(f"qT{i}", [HP, 1024], BF16) for i in range(2)]
        kT2 = [ar.alloc(f"kT{i}", [HP, 2048], BF16) for i in range(2)]
        vv2 = [ar.alloc(f"vv{i}", [16, WC], BF16) for i in range(2)]
        kst = [ar.alloc(f"kst{i}", [WC], F32) for i in range(2)]
        vst = [ar.alloc(f"vst{i}", [WC], F32) for i in range(2)]
        Eb = [[ar.alloc(f"E{g}{i}", [512], BF16) for i in range(2)] for g in range(2)]
        LKb = [[ar.alloc(f"LK{g}", [512], BF16)] * 2 for g in range(2)]
        XSb = [[ar.alloc(f"XS{g}", [512], BF16)] * 2 for g in range(2)]
        Wb = [[ar.alloc(f"W{g}", [512], BF16)] * 2 for g in range(2)]
        LKacc = [ar.alloc(f"LKacc{g}", [512], F32) for g in range(2)]
        LKaccb = [[ar.alloc(f"LKaccb{g}{i}", [512], BF16) for i in range(2)] for g in range(2)]
        BA = [0, 1]
        BB = [2, 2]
        BO = [3, 4]

        def evac_sample(dst_ap, dst_tile, ncol, scale=None):
            if scale is None:
                P.add("dve", lambda e: e.tensor_copy(out=dst_ap, in_=banks[SB][:, 0:ncol]),
                      reads=[bank_t[SB]], writes=[dst_tile])
            else:
                P.add("dve", lambda e: e.tensor_scalar(out=dst_ap, in0=banks[SB][:, 0:ncol], scalar1=scale, scalar2=None,
                                                       op0=ALU.mult),
                      reads=[bank_t[SB]], writes=[dst_tile])

        def attn_stage1(c, j, h, qT, kT, vv):
            (g, i, (kb, mi, bias), first, last) = c
            A = BA[g]
            E = Eb[g][i % 2]
            LK = LKb[g][i % 2]

            def zmm(e):
                ins = e.matmul(bk(A), lhsT=kT.ap[:, j, kb * 128:(kb + 1) * 128], rhs=qT.ap[:, j, g * 512:(g + 1) * 512],
                               start=True, stop=(mi is None))
                if mi is not None:
                    ins = e.matmul(bk(A), lhsT=ident_b.ap, rhs=masks.ap[:, mi, :], start=False, stop=True)
                return ins
            P.add("pe", zmm, reads=[kT.t((j, kb // 4)), qT.t((j, g)), ident_b.t()] + [masks.t(m) for m in range(4)],
                  writes=[bank_t[A]], name="z")
            P.add("act", lambda e: e.activation(out=E.ap, in_=bk(A), func=AF.Exp, bias=bias.ap[:, h:h + 1], scale=1.0),
                  reads=[bank_t[A], bias.t()], writes=[E.t()])
            P.add("act", lambda e: e.activation(out=LK.ap, in_=E.ap, func=AF.Ln, bias=1.0, scale=1.0),
                  reads=[E.t()], writes=[LK.t()])

        def attn_stage2(c, j, h, qT, kT, vv):
            (g, i, (kb, mi, bias), first, last) = c
            Bk = BB[g]
            LK = LKb[g][i % 2]
            XS = XSb[g][i % 2]
            accb_prev = LKaccb[g][(i + 1) % 2]
            accb_new = LKaccb[g][i % 2]
            acc = LKacc[g]

            def smm(e):
                ins = e.matmul(bk(Bk), lhsT=tri_b.ap, rhs=LK.ap, start=True, stop=first)
                if not first:
                    ins = e.matmul(bk(Bk), lhsT=ones_b.ap, rhs=accb_prev.ap, start=False, stop=True)
                return ins
            P.add("pe", smm, reads=[LK.t(), tri_b.t(), ones_b.t()] + ([] if first else [accb_prev.t()]),
                  writes=[bank_t[Bk]], name="sfx")
            if not last:
                if first:
                    P.add("dve", lambda e: e.tensor_copy(out=acc.ap, in_=LK.ap), reads=[LK.t()], writes=[acc.t()])
                else:
                    P.add("dve", lambda e: e.tensor_tensor(out=acc.ap, in0=acc.ap, in1=LK.ap, op=ALU.add),
                          reads=[LK.t(), acc.t()], writes=[acc.t()])
                P.add("pool", lambda e: e.tensor_copy(out=accb_new.ap, in_=acc.ap),
                      reads=[acc.t()], writes=[accb_new.t()])
            P.add("act", lambda e: e.activation(out=XS.ap, in_=bk(Bk), func=AF.Exp, scale=-1.0),
                  reads=[bank_t[Bk]], writes=[XS.t()])

        def attn_stage3(c, j, h, qT, kT, vv):
            (g, i, (kb, mi, bias), first, last) = c
            E = Eb[g][i % 2]
            XS = XSb[g][i % 2]
            W = Wb[g][i % 2]
            O = BO[g]
            P.add("dve", lambda e: e.tensor_tensor(out=W.ap, in0=E.ap, in1=XS.ap, op=ALU.mult),
                  reads=[E.t(), XS.t()], writes=[W.t()])
            P.add("pe", lambda e: e.matmul(bk(O), lhsT=vv.ap[:, kb, j * 128:(j + 1) * 128], rhs=W.ap,
                                           start=first, stop=last),
                  reads=[vv.t(kb), W.t()], writes=[bank_t[O]], name="av")
            if last:
                P.add("act", lambda e: e.copy(out=oT.ap[:, h, g * 512:(g + 1) * 512], in_=bk(O)),
                      reads=[bank_t[O]], writes=[oT.t((g, h))])

        prr = RR([5, 6])

        def proj_items(hq):
            qT, kT, vv = qT2[hq % 2], kT2[hq % 2], vv2[hq % 2]
            wvq, wtq = wq.next(T["w_in"], 0, 2048, OFF_Q + hq * WC, WC)
            for j in range(HP):
                b0, b1 = prr.next(), prr.next()
                fm_group(wvq, wtq, j, 16, mov_own + [mov_hs], [(bk(b0), b0), (bk(b1), b1), (banks[SB][:, j:j + 1], SB)], "q")
                for g, b in ((0, b0), (1, b1)):
                    P.add("act", lambda e, j=j, g=g, b=b, qT=qT: e.activation(out=qT.ap[:, j, g * 512:(g + 1) * 512], in_=bk(b),
                                                                       func=AF.Identity, scale=QSCALE),
                          reads=[bank_t[b]], writes=[qT.t((j, g))])
                yield
            evac_sample(qs.ap[:, hq * HP:hq * HP + HP], qs.t(hq), HP, scale=QSCALE)
            wq.done()
            wvk, wtk = wq.next(T["w_in"], 0, 2048, OFF_K + hq * WC, WC)
            for j in range(HP):
                b0, b1 = prr.next(), prr.next()
                fm_group(wvk, wtk, j, 16, mov_own + [mov_hs], [(bk(b0), b0), (bk(b1), b1), (banks[SB][:, j:j + 1], SB)], "k")
                for g, b in ((0, b0), (1, b1)):
                    P.add("act", lambda e, j=j, g=g, b=b, kT=kT: e.copy(out=kT.ap[:, j, g * 512:(g + 1) * 512], in_=bk(b)),
                          reads=[bank_t[b]], writes=[kT.t((j, g))])
                yield
                b2, b3 = prr.next(), prr.next()
                fm_group(wvk, wtk, j, 16, mov_oth, [(bk(b2), b2), (bk(b3), b3)], "ko")
                for g, b in ((0, b2), (1, b3)):
                    P.add("dve", lambda e, j=j, g=g, b=b, kT=kT: e.tensor_copy(out=kT.ap[:, j, 1024 + g * 512:1024 + (g + 1) * 512], in_=bk(b)),
                          reads=[bank_t[b]], writes=[kT.t((j, 2 + g))])
                yield
            evac_sample(ks.ap[:, hq * HP:hq * HP + HP], ks.t(hq), HP)
            wq.done()
            wvv, wtv = wq.next(T["w_in"], 0, 2048, OFF_V + hq * WC, WC)
            for tb in range(8):
                b = prr.next()
                pb = banks[b][:, :].bitcast(BF16)
                ks_ = kst[tb % 2]

                def ktr(e, tb=tb, pb=pb, kT=kT):
                    ins = None
                    for j in range(HP):
                        ins = e.transpose(out=pb[:, j * 128:(j + 1) * 128], in_=kT.ap[:, j, tb * 128:(tb + 1) * 128],
                                          identity=ident_b.ap)
                    return ins
                P.add("pe", ktr, reads=[kT.t((j, tb // 4)) for j in range(HP)] + [ident_b.t()], writes=[bank_t[b]], name="ktr")
                P.add("dve", lambda e, pb=pb, ks_=ks_: e.tensor_copy(out=ks_.ap, in_=pb[:, 0:WC]),
                      reads=[bank_t[b]], writes=[ks_.t()])
                kdst = T["k_own"][tb * 128:(tb + 1) * 128, hq * WC:(hq + 1) * WC]
                P.dma("sp", lambda e, kdst=kdst, ks_=ks_: [e.dma_start(out=kdst, in_=ks_.ap)], f"kst{tb % 2}", 1,
                      reads=[ks_.t()], final=True, name="kout")
                yield
            for tb in range(16):
                b = prr.next()
                src = hT if tb < 8 else hTo
                tbl = tb % 8

                def vmm(e, b=b, src=src, tbl=tbl, wvv=wvv):
                    ins = None
                    for k in range(KC):
                        ins = e.matmul(banks[b][:, 0:WC], lhsT=src.ap[:, k, tbl * 128:(tbl + 1) * 128], rhs=wvv[:, k, :],
                                       start=(k == 0), stop=(k == KC - 1))
                    return ins
                P.add("pe", vmm, reads=[wtv] + [src.t((tbl // 4, k)) for k in range(KC)], writes=[bank_t[b]], name="v")
                if tb < 8:
                    vs_ = vst[tb % 2]
                    P.add("dve", lambda e, b=b, vs_=vs_: e.tensor_copy(out=vs_.ap, in_=banks[b][:, 0:WC]),
                          reads=[bank_t[b]], writes=[vs_.t()])
                    P.add("act", lambda e, tb=tb, vs_=vs_, vv=vv: e.copy(out=vv.ap[:, tb, :], in_=vs_.ap),
                          reads=[vs_.t()], writes=[vv.t(tb)])
                    vdst = T["v_own"][tb * 128:(tb + 1) * 128, hq * WC:(hq + 1) * WC]
                    P.dma("sp", lambda e, vdst=vdst, vs_=vs_: [e.dma_start(out=vdst, in_=vs_.ap)], f"vst{tb % 2}", 1,
                          reads=[vs_.t()], final=True, name="vout")
                else:
                    P.add("act", lambda e, b=b, tb=tb, vv=vv: e.copy(out=vv.ap[:, tb, :], in_=banks[b][:, 0:WC]),
                          reads=[bank_t[b]], writes=[vv.t(tb)])
                yield
            for j in range(HP):
                fm_group(wvv, wtv, j, 16, [mov_hs], [(banks[SB][:, j:j + 1], SB)], "vs")
            evac_sample(vs.ap[:, hq * HP:hq * HP + HP], vs.t(hq), HP)
            wq.done()


        def attention_pair(hq, nxt):
            qT, kT, vv = qT2[hq % 2], kT2[hq % 2], vv2[hq % 2]
            for j in range(HP):
                h = hq * HP + j
                streams = []
                for g in range(2):
                    cl = []
                    for kb in range(4 * g + 3, 4 * g - 1, -1):
                        cl.append((kb, kb - 4 * g, bsb))
                    if g == 1:
                        for kb in range(3, -1, -1):
                            cl.append((kb, None, bsb))
                    for kb in range(7, -1, -1):
                        cl.append((8 + kb, None, bsb_o))
                    streams.append(cl)
                seq = []
                for i in range(max(len(streams[0]), len(streams[1]))):
                    for g in range(2):
                        if i < len(streams[g]):
                            seq.append((g, i, streams[g][i], i == 0, i == len(streams[g]) - 1))
                n = len(seq)
                for it in range(n + 2):
                    if nxt is not None and it % 2 == 1:
                        next(nxt, None)
                    if it < n:
                        attn_stage1(seq[it], j, h, qT, kT, vv)
                    if 0 <= it - 1 < n:
                        attn_stage2(seq[it - 1], j, h, qT, kT, vv)
                    if 0 <= it - 2 < n:
                        attn_stage3(seq[it - 2], j, h, qT, kT, vv)


        g0 = proj_items(0)
        for _ in g0:
            pass
        for hq in range(NHG):
            nxt = proj_items(hq + 1) if hq + 1 < NHG else None
            attention_pair(hq, nxt)
            if nxt is not None:
                for _ in nxt:
                    pass

        for lst in (Eb, LKaccb):
            for l2 in lst:
                for b_ in l2:
                    ar.free(b_)
        for lst in (LKb, XSb, Wb):
            for l2 in lst:
                ar.free(l2[0])
        for b_ in LKacc + qT2 + kT2 + vv2 + kst + vst:
            ar.free(b_)

        P.dma("sp", lambda e: [e.dma_start(out=T["k_s"].rearrange("(h p) -> p h", p=128), in_=ks.ap, **NCD),
                               e.dma_start(out=T["v_s"].rearrange("(h p) -> p h", p=128), in_=vs.ap, **NCD)],
              "kvs", 2, reads=[ks.t(i) for i in range(8)] + [vs.t(i) for i in range(8)], final=True)

        qbf = ar.alloc("qbf", [2048], F32)
        qb = ar.alloc("qb", [2048], BF16)
        zs = ar.alloc("zs", [128, 16], F32)
        NKS = 4
        kpg = [ar.alloc(f"kpg{i}", [2048], BF16) for i in range(NKS)]
        ktmp2 = [ar.alloc(f"ktmp{i}", [2048], F32) for i in range(2)]
        QS_T = Tile("qscr")
        P.dma("sp", lambda e: [e.dma_start(out=T["q_scr"].rearrange("(h p) -> p h", p=128), in_=qs.ap, **NCD)],
              "qscr", 1, reads=[qs.t(i) for i in range(8)], writes=[QS_T], name="qscr_w")
        P.dma("sp", lambda e: [e.dma_start(out=qbf.ap, in_=T["q_scr"].partition_broadcast(128))], "qb", 1,
              reads=[QS_T], writes=[qbf.t()], name="qscr_r")
        P.add("dve", lambda e: e.tensor_copy(out=qb.ap, in_=qbf.ap), reads=[qbf.t()], writes=[qb.t()])
        krows = T["cache_k"].rearrange("n p f -> (n p) f")
        vrows = T["cache_v"].rearrange("n p f -> (n p) f")

        for pg in range(NPAGES):
            slot = pg % NKS
            P.dma("pool", lambda e, pg=pg, slot=slot: [e.indirect_dma_start(
                out=kpg[slot].ap, out_offset=None, in_=krows,
                in_offset=bass.IndirectOffsetOnAxis(ap=pidx.ap[:, pg:pg + 1], axis=0))],
                f"kpg{slot}", 1, reads=[pidx.t()], writes=[kpg[slot].t()], name="kpage")
            kt_ = ktmp2[pg % 2]
            P.add("dve", lambda e, slot=slot, kt_=kt_: e.tensor_tensor(out=kt_.ap, in0=kpg[slot].ap, in1=qb.ap, op=ALU.mult),
                  reads=[kpg[slot].t(), qb.t()], writes=[kt_.t()])
            P.add("dve", lambda e, pg=pg, kt_=kt_: e.tensor_reduce(out=zs.ap[:, pg, :], in_=kt_.ap.rearrange("p (h d) -> p h d", h=16),
                                                                   axis=AX.X, op=ALU.add),
                  reads=[kt_.t()], writes=[zs.t(pg)])
        for b_ in kpg + ktmp2 + [qb, qbf]:
            ar.free(b_)
        zall = [zs.t(pg) for pg in range(NPAGES)]
        Es = ar.alloc("Es", [128, 16], F32)
        LKs = ar.alloc("LKs", [128, 16], F32)
        Ssum = ar.alloc("Ssum", [128, 16], F32)
        PTa = ar.alloc("PTa", [128, 16], F32)
        PTb = ar.alloc("PTb", [128, 16], F32)
        PT0 = ar.alloc("PT0", [128, 16], F32)
        wsb = ar.alloc("wsb", [128, 16], BF16)
        ZS = Tile("zs_all")
        for hh in range(H):
            P.add("act", lambda e, hh=hh: e.activation(out=Es.ap[:, :, hh], in_=zs.ap[:, :, hh], func=AF.Exp,
                                                       bias=bsb.ap[:, hh:hh + 1], scale=1.0),
                  reads=zall + [bsb.t()], writes=[Es.t(hh)])
        eall = [Es.t(hh) for hh in range(H)]
        P.add("act", lambda e: e.activation(out=LKs.ap, in_=Es.ap, func=AF.Ln, bias=1.0, scale=1.0), reads=eall, writes=[LKs.t()])
        LKf = LKs.ap.rearrange("p a b -> p (a b)")
        for n_ in range(4):
            P.add("pe", lambda e, n_=n_: e.matmul(bk(n_), lhsT=tri_f.ap, rhs=LKf[:, n_ * 512:(n_ + 1) * 512], start=True, stop=True),
                  reads=[LKs.t(), tri_f.t()], writes=[bank_t[n_]], name="s_tri")
            P.add("pe", lambda e, n_=n_: e.matmul(bk(4 + n_), lhsT=ones_f.ap, rhs=LKf[:, n_ * 512:(n_ + 1) * 512], start=True, stop=True),
                  reads=[LKs.t(), ones_f.t()], writes=[bank_t[4 + n_]], name="s_tot")
        Sf = Ssum.ap.rearrange("p a b -> p (a b)")
        PTaf = PTa.ap.rearrange("p a b -> p (a b)")
        PT0f = PT0.ap.rearrange("p a b -> p (a b)")
        for n_ in range(4):
            P.add("act", lambda e, n_=n_: e.copy(out=Sf[:, n_ * 512:(n_ + 1) * 512], in_=bk(n_)),
                  reads=[bank_t[n_]], writes=[Ssum.t(n_)])
            P.add("dve", lambda e, n_=n_: e.tensor_copy(out=PTaf[:, n_ * 512:(n_ + 1) * 512], in_=bk(4 + n_)),
                  reads=[bank_t[4 + n_]], writes=[PTa.t(n_)])
        P.add("pool", lambda e: e.tensor_copy(out=PT0f, in_=PTaf), reads=[PTa.t(n_) for n_ in range(4)], writes=[PT0.t()])
        cur, oth = PTa, PTb
        cur_t = [PTa.t(n_) for n_ in range(4)]
        sh = 1
        while sh < NPAGES:
            def scan(e, cur=cur, oth=oth, sh=sh):
                e.tensor_tensor(out=oth.ap[:, 0:NPAGES - sh, :], in0=cur.ap[:, 0:NPAGES - sh, :], in1=cur.ap[:, sh:NPAGES, :], op=ALU.add)
                return e.tensor_copy(out=oth.ap[:, NPAGES - sh:NPAGES, :], in_=cur.ap[:, NPAGES - sh:NPAGES, :])
            P.add("dve", scan, reads=cur_t, writes=[oth.t()])
            cur, oth = oth, cur
            cur_t = [cur.t()]
            sh *= 2
        curf = cur.ap.rearrange("p a b -> p (a b)")
        P.add("dve", lambda e: e.tensor_tensor(out=Sf, in0=Sf, in1=curf, op=ALU.add),
              reads=cur_t + [Ssum.t(n_) for n_ in range(4)], writes=[Ssum.t()])
        P.add("dve", lambda e: e.tensor_tensor(out=Sf, in0=Sf, in1=PT0f, op=ALU.subtract), reads=[Ssum.t(), PT0.t()], writes=[Ssum.t()])
        P.add("act", lambda e: e.activation(out=Sf, in_=Sf, func=AF.Exp, scale=-1.0), reads=[Ssum.t()], writes=[Ssum.t()])
        P.add("dve", lambda e: e.tensor_tensor(out=wsb.ap.rearrange("p a b -> p (a b)"), in0=Sf,
                                               in1=Es.ap.rearrange("p a b -> p (a b)"), op=ALU.mult),
              reads=[Ssum.t()] + eall, writes=[wsb.t()])
        for b_ in (Es, LKs, PTa, PTb, PT0):
            ar.free(b_)
        vpg = [ar.alloc(f"vpg{i}", [2048], BF16) for i in range(3)]
        for pg in range(NPAGES):
            slot = pg % 3

            P.dma("pool", lambda e, pg=pg, slot=slot: [e.indirect_dma_start(
                out=vpg[slot].ap, out_offset=None, in_=vrows,
                in_offset=bass.IndirectOffsetOnAxis(ap=pidx.ap[:, pg:pg + 1], axis=0))],
                f"vpg{slot}", 1, reads=[pidx.t()], writes=[vpg[slot].t()], name="vpage")

            def vacc(e, pg=pg, slot=slot):
                ins = None
                for n_ in range(4):
                    ins = e.matmul(banks[n_][0:16, :], lhsT=wsb.ap[:, pg, :], rhs=vpg[slot].ap[:, n_ * 512:(n_ + 1) * 512],
                                   start=(pg == 0), stop=(pg == NPAGES - 1))
                return ins
            P.add("pe", vacc, reads=[wsb.t(), vpg[slot].t()], writes=[bank_t[n_] for n_ in range(4)], name="s_av")
        oacc = ar.alloc("oacc", [2048], F32)
        dmask = ar.alloc("dmask", [2048], F32)
        P.add("pool", lambda e: e.memset(dmask.ap[0:16, :], 1.0), writes=[dmask.t()])
        P.add("pool", lambda e: e.affine_select(out=dmask.ap[0:16, :], in_=dmask.ap[0:16, :], pattern=[[1, 2048]],
                                                compare_op=ALU.is_ge, fill=0.0, base=0, channel_multiplier=-128),
              reads=[dmask.t()], writes=[dmask.t()])
        P.add("pool", lambda e: e.affine_select(out=dmask.ap[0:16, :], in_=dmask.ap[0:16, :], pattern=[[-1, 2048]],
                                                compare_op=ALU.is_ge, fill=0.0, base=127, channel_multiplier=128),
              reads=[dmask.t()], writes=[dmask.t()])
        for n_ in range(4):
            P.add("dve", lambda e, n_=n_: e.tensor_tensor(out=oacc.ap[0:16, n_ * 512:(n_ + 1) * 512], in0=banks[n_][0:16, :],
                                                          in1=dmask.ap[0:16, n_ * 512:(n_ + 1) * 512], op=ALU.mult),
                  reads=[bank_t[n_], dmask.t()], writes=[oacc.t(n_)])

        def odiag(e):
            ins = None
            for hh in range(H):
                ins = e.matmul(banks[SB][:, hh:hh + 1], lhsT=oacc.ap[0:16, hh * 128:(hh + 1) * 128], rhs=ones_f.ap[0:16, 0:1],
                               start=True, stop=True)
            return ins
        P.add("pe", odiag, reads=[oacc.t(n_) for n_ in range(4)] + [ones_f.t()], writes=[bank_t[SB]], name="odiag")
        P.add("dve", lambda e: e.tensor_copy(out=os_b.ap, in_=banks[SB][:, 0:16]), reads=[bank_t[SB]], writes=[os_b.t()])
        for b_ in vpg + [oacc, dmask, zs, Ssum, wsb]:
            ar.free(b_)

        hx = ar.alloc("hx", [16, 3], BF16)
        P.add("dve", lambda e: e.tensor_copy(out=hx.ap[:, :, 0:2], in_=hTo.ap[:, :, 1022:1024]),
              reads=[hTo.t((1, k)) for k in range(KC)], writes=[hx.t()])
        P.add("dve", lambda e: e.tensor_copy(out=hx.ap[:, :, 2], in_=hs.ap), reads=[hs.t(), hx.t()], writes=[hx.t()])
        ar.free(hTo)
        oconvT = ar.alloc("oconvT", [8, 1024], BF16, at=97)
        mov_hx = (lambda k: hx.ap[:, k, :], [hx.t()])
        ccs4 = ar.alloc("ccs4", [4, 1024], F32)
        U4 = ar.alloc("U4", [4, 1026], F32)
        for jq in range(2):
            wvc, wtc = wq.next(T["w_in"], 0, 2048, OFF_CC + jq * 512, 512)
            for j in range(4):
                b0, b1 = acc_rr.next(), acc_rr.next()
                fm_group(wvc, wtc, j, 16, mov_own + [mov_hx], [(bk(b0), b0), (bk(b1), b1), (banks[SB][:, 0:3], SB)], "cc")
                for g, b in ((0, b0), (1, b1)):
                    P.add("act", lambda e, g=g, b=b, j=j: e.copy(out=ccs4.ap[:, j, g * 512:(g + 1) * 512], in_=bk(b)),
                          reads=[bank_t[b]], writes=[ccs4.t((j, g))])
                P.add("dve", lambda e, j=j: e.tensor_copy(out=smc.ap[:, j, 0:3], in_=banks[SB][:, 0:3]), reads=[bank_t[SB]], writes=[smc.t(j)])
            wq.done()
            wvh, wth = wq.next(T["w_in"], 0, 2048, OFF_CH + jq * 512, 512)
            for j in range(4):
                jj = jq * 4 + j
                b0, b1 = acc_rr.next(), acc_rr.next()
                fm_group(wvh, wth, j, 16, mov_own + [mov_hx], [(bk(b0), b0), (bk(b1), b1), (banks[SB][:, 0:3], SB)], "ch")
                for g, b in ((0, b0), (1, b1)):
                    P.add("dve", lambda e, g=g, b=b, j=j: e.tensor_tensor(
                        out=U4.ap[:, j, 2 + g * 512:2 + (g + 1) * 512], in0=bk(b), in1=ccs4.ap[:, j, g * 512:(g + 1) * 512], op=ALU.mult),
                        reads=[bank_t[b], ccs4.t((j, g))], writes=[U4.t((j, g))])
                P.add("dve", lambda e, j=j: e.tensor_tensor(out=smc.ap[:, j, 4:7], in0=banks[SB][:, 0:3], in1=smc.ap[:, j, 0:3], op=ALU.mult),
                      reads=[bank_t[SB], smc.t(j)], writes=[smc.t(j)])
                P.add("dve", lambda e, j=j: e.tensor_scalar(out=U4.ap[:, j, 0:2], in0=smc.ap[:, j, 4:6], scalar1=flags.ap[:, 0:1],
                                                            scalar2=None, op0=ALU.mult),
                      reads=[smc.t(j), flags.t()], writes=[U4.t((j, 2))])
                ut = [U4.t((j, 0)), U4.t((j, 1)), U4.t((j, 2))]
                cvt = [ccs4.t((j, 0)), ccs4.t((j, 1))]
                P.add("dve", lambda e, j=j, jj=jj: e.tensor_scalar(out=ccs4.ap[:, j, :], in0=U4.ap[:, j, 0:1024], scalar1=cw.ap[:, jj, 0:1],
                                                                   scalar2=None, op0=ALU.mult),
                      reads=ut + [cw.t()], writes=cvt)
                for tap in (1, 2):
                    P.add("dve", lambda e, j=j, jj=jj, tap=tap: e.scalar_tensor_tensor(
                        out=ccs4.ap[:, j, :], in0=U4.ap[:, j, tap:tap + 1024], scalar=cw.ap[:, jj, tap:tap + 1], in1=ccs4.ap[:, j, :],
                        op0=ALU.mult, op1=ALU.add),
                        reads=ut + [cw.t()] + cvt, writes=cvt)
                P.add("dve", lambda e, j=j, jj=jj: e.tensor_copy(out=cpst.ap[:, jj, :], in_=U4.ap[:, j, 1024:1026]),
                      reads=ut, writes=[cpst.t(jj)])
                P.add("dve", lambda e, j=j, jj=jj: e.tensor_tensor(out=smc.ap[:, j, 8:10], in0=stc.ap[:, jj, :], in1=cw.ap[:, jj, 0:2], op=ALU.mult),
                      reads=[stc.t(), cw.t(), smc.t(j)], writes=[smc.t(j)])
                P.add("dve", lambda e, j=j: e.tensor_tensor(out=smc.ap[:, j, 10:11], in0=smc.ap[:, j, 8:9], in1=smc.ap[:, j, 9:10], op=ALU.add),
                      reads=[smc.t(j)], writes=[smc.t(j)])
                P.add("dve", lambda e, j=j, jj=jj: e.scalar_tensor_tensor(out=smc.ap[:, j, 11:12], in0=smc.ap[:, j, 6:7], scalar=cw.ap[:, jj, 2:3],
                                                                          in1=smc.ap[:, j, 10:11], op0=ALU.mult, op1=ALU.add),
                      reads=[smc.t(j), cw.t()], writes=[smc.t(j)])
                P.add("dve", lambda e, jj=jj: e.tensor_copy(out=csst.ap[:, jj, 0:1], in_=stc.ap[:, jj, 1:2]),
                      reads=[stc.t()], writes=[csst.t((jj, 0))])
                P.add("dve", lambda e, j=j, jj=jj: e.tensor_copy(out=csst.ap[:, jj, 1:2], in_=smc.ap[:, j, 6:7]),
                      reads=[smc.t(j)], writes=[csst.t((jj, 1))])
            wq.done()
            wvb, wtb = wq.next(T["w_in"], 0, 2048, OFF_CB + jq * 512, 512)
            for j in range(4):
                jj = jq * 4 + j
                b0, b1 = acc_rr.next(), acc_rr.next()
                fm_group(wvb, wtb, j, 16, mov_own + [mov_hs], [(bk(b0), b0), (bk(b1), b1), (banks[SB][:, 0:1], SB)], "cb")
                for g, b in ((0, b0), (1, b1)):
                    P.add("dve", lambda e, g=g, b=b, j=j, jj=jj: e.tensor_tensor(
                        out=oconvT.ap[:, jj, g * 512:(g + 1) * 512], in0=bk(b), in1=ccs4.ap[:, j, g * 512:(g + 1) * 512], op=ALU.mult),
                        reads=[bank_t[b], ccs4.t((j, g))], writes=[oconvT.t((g, jj))])
                P.add("dve", lambda e, j=j, jj=jj: e.tensor_tensor(out=ocs_b.ap[:, jj:jj + 1], in0=banks[SB][:, 0:1], in1=smc.ap[:, j, 11:12], op=ALU.mult),
                      reads=[bank_t[SB], smc.t(j)], writes=[ocs_b.t(jj)])
            wq.done()
        P.dma("sp", lambda e: [e.dma_start(out=fm_vec(T["conv_p"][jt, :]), in_=cpst.ap[:, :, jt], **NCD) for jt in range(2)]
              + [e.dma_start(out=fm_vec(T["conv_s"][jt, :]), in_=csst.ap[:, :, jt], **NCD) for jt in range(2)],
              "convout", 4, reads=[cpst.t(jj) for jj in range(8)] + [csst.t((jj, i)) for jj in range(8) for i in range(2)], final=True)
        for b_ in (ccs4, U4, hx):
            ar.free(b_)

        mergedT = ar.alloc("mergedT", [16, 1024], BF16, at=0)
        sg4 = ar.alloc("sg4", [4, 1024], F32, at=32)
        m14 = ar.alloc("m14", [4, 1024], F32, at=48)
        mov_o = [(lambda k, g=g: oT.ap[:, k, g * 512:(g + 1) * 512], [oT.t((g, k)) for k in range(H)]) for g in range(2)]
        mov_oc = [(lambda k, g=g: oconvT.ap[:, k, g * 512:(g + 1) * 512], [oconvT.t((g, k)) for k in range(8)]) for g in range(2)]
        mov_os = (lambda k: os_b.ap[:, k:k + 1], [os_b.t()])
        mov_ocs = (lambda k: ocs_b.ap[:, k:k + 1], [ocs_b.t(k) for k in range(8)])

        def gate_tile(w, off, cq):
            wv_, wt_ = wq.next(w, 0, 2048, off + cq * 512, 512)
            for j in range(4):
                b0, b1 = acc_rr.next(), acc_rr.next()
                fm_group(wv_, wt_, j, 16, mov_own + [mov_hs], [(bk(b0), b0), (bk(b1), b1), (banks[SB][:, 0:1], SB)], "gate")
                for g, b in ((0, b0), (1, b1)):
                    P.add("act", lambda e, g=g, b=b, j=j: e.activation(out=sg4.ap[:, j, g * 512:(g + 1) * 512], in_=bk(b), func=AF.Sigmoid),
                          reads=[bank_t[b]], writes=[sg4.t((j, g))])
                P.add("act", lambda e, j=j: e.activation(out=smc.ap[:, j, 0:1], in_=banks[SB][:, 0:1], func=AF.Sigmoid),
                      reads=[bank_t[SB]], writes=[smc.t(j)])
            wq.done()

        for cq in range(4):
            gate_tile(T["w_in"], OFF_GA, cq)
            wva, wta = wq.next(T["w_attn_out"], 0, 2048, cq * 512, 512)
            for j in range(4):
                b0, b1 = acc_rr.next(), acc_rr.next()
                fm_group(wva, wta, j, 16, mov_o + [mov_os], [(bk(b0), b0), (bk(b1), b1), (banks[SB][:, 0:1], SB)], "A")
                for g, b in ((0, b0), (1, b1)):
                    P.add("dve", lambda e, g=g, b=b, j=j: e.tensor_tensor(out=m14.ap[:, j, g * 512:(g + 1) * 512], in0=bk(b),
                                                                          in1=sg4.ap[:, j, g * 512:(g + 1) * 512], op=ALU.mult),
                          reads=[bank_t[b], sg4.t((j, g))], writes=[m14.t((j, g))])
                P.add("dve", lambda e, j=j: e.tensor_tensor(out=smc.ap[:, j, 1:2], in0=banks[SB][:, 0:1], in1=smc.ap[:, j, 0:1], op=ALU.mult),
                      reads=[bank_t[SB], smc.t(j)], writes=[smc.t(j)])
            wq.done()
            gate_tile(T["w_in"], OFF_GB, cq)
            wvc, wtc = wq.next(T["w_conv_out"], 0, 1024, cq * 512, 512)
            for j in range(4):
                c = cq * 4 + j
                b0, b1 = acc_rr.next(), acc_rr.next()
                fm_group(wvc, wtc, j, 8, mov_oc + [mov_ocs], [(bk(b0), b0), (bk(b1), b1), (banks[SB][:, 0:1], SB)], "C")
                for g, b in ((0, b0), (1, b1)):
                    P.add("dve", lambda e, g=g, b=b, j=j: e.tensor_tensor(out=sg4.ap[:, j, g * 512:(g + 1) * 512], in0=bk(b),
                                                                          in1=sg4.ap[:, j, g * 512:(g + 1) * 512], op=ALU.mult),
                          reads=[bank_t[b], sg4.t((j, g))], writes=[sg4.t((j, g))])
                    P.add("dve", lambda e, g=g, j=j, c=c: e.tensor_tensor(out=mergedT.ap[:, c, g * 512:(g + 1) * 512],
                                                                          in0=sg4.ap[:, j, g * 512:(g + 1) * 512],
                                                                          in1=m14.ap[:, j, g * 512:(g + 1) * 512], op=ALU.add),
                          reads=[sg4.t((j, g)), m14.t((j, g))], writes=[mergedT.t((g, c))])
                P.add("dve", lambda e, j=j: e.tensor_tensor(out=smc.ap[:, j, 2:3], in0=banks[SB][:, 0:1], in1=smc.ap[:, j, 0:1], op=ALU.mult),
                      reads=[bank_t[SB], smc.t(j)], writes=[smc.t(j)])
                P.add("dve", lambda e, j=j, c=c: e.tensor_tensor(out=mgs_b.ap[:, c:c + 1], in0=smc.ap[:, j, 1:2], in1=smc.ap[:, j, 2:3], op=ALU.add),
                      reads=[smc.t(j)], writes=[mgs_b.t(c)])
            wq.done()
        for b_ in (sg4, m14, oT, oconvT):
            ar.free(b_)

        ar.free(hT)
        xT = ar.alloc("xT", [16, 1024], F32, at=65)
        xin5 = ar.alloc("xin5", [4, 2048], F32, at=129)
        for g in range(2):
            load_x(T["x_own"], g, xin5)
            transpose_group(xin5, lambda k, g=g: xT.ap[:, k, g * 512:(g + 1) * 512], lambda k, g=g: xT.t((g, k)))
        ar.free(xin5)
        mov_m = [(lambda k, g=g: mergedT.ap[:, k, g * 512:(g + 1) * 512], [mergedT.t((g, k)) for k in range(KC)]) for g in range(2)]
        mov_ms = (lambda k: mgs_b.ap[:, k:k + 1], [mgs_b.t(k) for k in range(KC)])

        def resid_update(b, c, g, gate_fn):
            P.add("dve", lambda e: e.scalar_tensor_tensor(out=xT.ap[:, c, g * 512:(g + 1) * 512], in0=bk(b), scalar=gate_fn(c, 0),
                                                          in1=xT.ap[:, c, g * 512:(g + 1) * 512], op0=ALU.mult, op1=ALU.add),
                  reads=[bank_t[b], xT.t((g, c)), MODT], writes=[xT.t((g, c))])

        def resid_update_s(col, c, gate_fn):
            P.add("dve", lambda e: e.scalar_tensor_tensor(out=xs.ap[:, c:c + 1], in0=banks[SB][:, col:col + 1], scalar=gate_fn(c, 1),
                                                          in1=xs.ap[:, c:c + 1], op0=ALU.mult, op1=ALU.add),
                  reads=[bank_t[SB], xs.t(), MODT], writes=[xs.t()])
        for cq in range(4):
            wvo, wto = wq.next(T["w_out"], 0, 2048, cq * 512, 512)
            for j in range(4):
                c = cq * 4 + j
                b0, b1 = acc_rr.next(), acc_rr.next()
                fm_group(wvo, wto, j, 16, mov_m + [mov_ms], [(bk(b0), b0), (bk(b1), b1), (banks[SB][:, 0:1], SB)], "mix")
                resid_update(b0, c, 0, GATE1)
                resid_update(b1, c, 1, GATE1)
                resid_update_s(0, c, GATE1)
            wq.done()
        ar.free(mergedT)

        hT2 = ar.alloc("hT2", [16, 1024], BF16, at=129)
        sqb2 = [ar.alloc(f"sqb2{i}", [512], BF16) for i in range(2)]
        rstd2 = ar.alloc("rstd2", [512], F32)
        ntmp2 = [ar.alloc(f"ntmp2{i}", [512], F32) for i in range(2)]
        mov_own2 = [(lambda k, g=g: hT2.ap[:, k, g * 512:(g + 1) * 512], [hT2.t((g, k)) for k in range(KC)]) for g in range(2)]
        for g in range(2):
            group_norm_stats(lambda k, g=g: xT.ap[:, k, g * 512:(g + 1) * 512], lambda k, g=g: [xT.t((g, k))], sqb2, rstd2, f"n2_{g}")
            norm_apply(lambda k, g=g: xT.ap[:, k, g * 512:(g + 1) * 512], lambda k, g=g: xT.t((g, k)), hT2, g,
                       lambda k: A2.ap[:, k, 0:1], lambda k: B2(k, 0), rstd2, ntmp2)
        sample_norm(A2.ap[:, :, 1], modT.ap[:, 48:64, 1], hs, "sn2")
        for b_ in sqb2 + ntmp2 + [rstd2]:
            ar.free(b_)

        gT = ar.alloc("gT", [22, 1024], BF16, at=0)
        sl4 = ar.alloc("sl4", [4, 1024], F32, at=44)
        for half in range(2):
            for fq in range(6):
                ncol = 512 if fq < 5 else 256
                nch = ncol // 128
                c0 = half * 2816 + fq * 512
                wva_, wta_ = wq.next(T["w_ffn_in"], 0, 2048, c0, ncol)
                for j in range(nch):
                    a0, a1 = acc_rr.next(), acc_rr.next()
                    fm_group(wva_, wta_, j, 16, mov_own2 + [mov_hs], [(bk(a0), a0), (bk(a1), a1), (banks[SB][:, 0:1], SB)], "ffa")
                    for g, b in ((0, a0), (1, a1)):
                        P.add("act", lambda e, g=g, b=b, j=j: e.activation(out=sl4.ap[:, j, g * 512:(g + 1) * 512], in_=bk(b), func=AF.Silu),
                              reads=[bank_t[b]], writes=[sl4.t((j, g))])
                    P.add("act", lambda e, j=j: e.activation(out=smc.ap[:, j, 0:1], in_=banks[SB][:, 0:1], func=AF.Silu),
                          reads=[bank_t[SB]], writes=[smc.t(j)])
                wq.done()
                wvb_, wtb_ = wq.next(T["w_ffn_in"], 0, 2048, DFF + c0, ncol)
                for j in range(nch):
                    jl = fq * 4 + j
                    jj = half * 22 + jl
                    b0, b1 = acc_rr.next(), acc_rr.next()
                    fm_group(wvb_, wtb_, j, 16, mov_own2 + [mov_hs], [(bk(b0), b0), (bk(b1), b1), (banks[SB][:, 0:1], SB)], "ffb")
                    for g, b in ((0, b0), (1, b1)):
                        P.add("dve", lambda e, g=g, b=b, j=j, jl=jl: e.tensor_tensor(out=gT.ap[:, jl, g * 512:(g + 1) * 512], in0=bk(b),
                                                                                     in1=sl4.ap[:, j, g * 512:(g + 1) * 512], op=ALU.mult),
                              reads=[bank_t[b], sl4.t((j, g))], writes=[gT.t((g, jl))])
                    P.add("dve", lambda e, j=j, jj=jj: e.tensor_tensor(out=gs_b.ap[:, jj:jj + 1], in0=banks[SB][:, 0:1], in1=smc.ap[:, j, 0:1], op=ALU.mult),
                          reads=[bank_t[SB], smc.t(j)], writes=[gs_b.t(jj)])
                wq.done()
            mov_g = [(lambda k, g=g: gT.ap[:, k, g * 512:(g + 1) * 512], [gT.t((g, k)) for k in range(22)]) for g in range(2)]
            mov_gs = (lambda k, half=half: gs_b.ap[:, half * 22 + k:half * 22 + k + 1], [gs_b.t(half * 22 + k) for k in range(22)])
            for c2 in range(8):
                wvf, wtf = wq.next(T["w_ffn_out"], half * 2816, 2816, c2 * 256, 256)
                for j in range(2):
                    c = c2 * 2 + j
                    b0, b1 = acc_rr.next(), acc_rr.next()
                    fm_group(wvf, wtf, j, 22, mov_g + [mov_gs], [(bk(b0), b0), (bk(b1), b1), (banks[SB][:, 0:1], SB)], "ffo")
                    resid_update(b0, c, 0, GATE2)
                    resid_update(b1, c, 1, GATE2)
                    resid_update_s(0, c, GATE2)
                wq.done()
        for b_ in (sl4, gT):
            ar.free(b_)
        ar.free(hT2)

        yst = ar.alloc("yst", [4, 2048], F32)
        sqb3 = [ar.alloc(f"sqb3{i}", [512], BF16) for i in range(2)]
        rstd3 = ar.alloc("rstd3", [512], F32)
        ntmp3 = [ar.alloc(f"ntmp3{i}", [512], F32) for i in range(2)]
        yT = [ar.alloc(f"yT{i}", [512], F32) for i in range(2)]
        for g in range(2):
            group_norm_stats(lambda k, g=g: xT.ap[:, k, g * 512:(g + 1) * 512], lambda k, g=g: [xT.t((g, k))], sqb3, rstd3, f"n3_{g}")
            for c in range(KC):
                tb_, y_ = ntmp3[c % 2], yT[c % 2]
                P.add("dve", lambda e, c=c, g=g, tb_=tb_: e.tensor_tensor(out=tb_.ap, in0=xT.ap[:, c, g * 512:(g + 1) * 512], in1=rstd3.ap, op=ALU.mult),
                      reads=[xT.t((g, c)), rstd3.t()], writes=[tb_.t()])
                P.add("act", lambda e, c=c, tb_=tb_, y_=y_: e.activation(out=y_.ap, in_=tb_.ap, func=AF.Identity, scale=gfin.ap[:, c:c + 1]),
                      reads=[tb_.t(), gfin.t()], writes=[y_.t()])
                b = acc_rr.next()

                def ytr(e, y_=y_, b=b):
                    ins = None
                    for j in range(4):
                        ins = e.transpose(out=banks[b][:, j * 128:(j + 1) * 128], in_=y_.ap[:, j * 128:(j + 1) * 128], identity=ident_f.ap)
                    return ins
                P.add("pe", ytr, reads=[y_.t(), ident_f.t()], writes=[bank_t[b]], name="ytr")
                P.add("dve", lambda e, c=c, b=b: e.tensor_copy(out=yst.ap[:, :, c * 128:(c + 1) * 128],
                                                                in_=banks[b][:, :].rearrange("p (a f) -> p a f", a=4)),
                      reads=[bank_t[b]], writes=[yst.t(c)])
            ydst = T["y_own"][g * 512:(g + 1) * 512, :].rearrange("(b p) f -> p b f", p=128)
            P.dma("sp", lambda e, ydst=ydst: [e.dma_start(out=ydst, in_=yst.ap)], "yst", 1,
                  reads=[yst.t(c) for c in range(KC)], final=True, name="yout")
        ys = ar.alloc("ys", [16], F32)
        sample_norm(gfin.ap, None, ys, "sn3")
        P.dma("sp", lambda e: [e.dma_start(out=fm_vec(T["y_s"]), in_=ys.ap, **NCD)], "ys", 1, reads=[ys.t()], final=True)

        counts = P.emit(st)
        build_core_program.info = dict(signals=counts, peak=ar.peak, nops=len(P.ops))
        return wq.rec


IN_SPECS = [
    ("x_own", [1024, 2048], F32), ("x_oth", [1024, 2048], F32), ("x_s", [2048], F32),
    ("c_p", [2048], F32), ("c_s", [2048], F32), ("flags", [128, 2], F32),
    ("cache_k", [NPOOL, 128, 2048], F32), ("cache_v", [NPOOL, 128, 2048], F32),
    ("page_tbl", [1, NPAGES], I32), ("state_conv", [2, 1024], F32),
    ("g_mix", [2048], F32), ("g_ffn", [2048], F32), ("g_final", [2048], F32),
    ("w_mod", [2048, 12288], F32), ("b_mod", [12288], F32), ("w_in", [2048, D_IN], F32),
    ("b_sb", [16], F32), ("conv_w", [3, 1024], F32), ("w_attn_out", [2048, 2048], F32),
    ("w_conv_out", [1024, 2048], F32), ("w_out", [2048, 2048], F32),
    ("w_ffn_in", [2048, 2 * DFF], F32), ("w_ffn_out", [DFF, 2048], F32),
]
OUT_SPECS = [
    ("y_own", [1024, 2048]), ("y_s", [2048]), ("k_own", [1024, 2048]), ("v_own", [1024, 2048]),
    ("conv_p", [2, 1024]), ("k_s", [2048]), ("v_s", [2048]), ("conv_s", [2, 1024]),
]


def build_nc():
    plan = None
    for it in range(2):
        nc = bass.Bass("TRN2", target_bir_lowering=False)
        T = {}
        for (n, shp, dt) in IN_SPECS:
            T[n] = nc.dram_tensor(n, shp, dt, kind="ExternalInput").ap()
        for (n, shp) in OUT_SPECS:
            T[n] = nc.dram_tensor(n, shp, F32, kind="ExternalOutput").ap()
        T["q_scr"] = nc.dram_tensor("q_scr", [2048], F32, kind="Internal").ap()
        plan = build_core_program(nc, T, plan)
    return nc


def make_in_maps(inp):
    f = lambda a: np.ascontiguousarray(np.asarray(a, dtype=np.float32))
    shared = {
        "cache_k": f(inp["cache_k"][0]).reshape(NPOOL, 128, 2048), "cache_v": f(inp["cache_v"][0]).reshape(NPOOL, 128, 2048),
        "g_mix": f(inp["g_mix"][0]), "g_ffn": f(inp["g_ffn"][0]), "g_final": f(inp["g_final"]),
        "w_mod": f(inp["w_mod"][0]), "b_mod": f(inp["b_mod"][0]), "w_in": f(inp["w_in"][0]), "b_sb": f(inp["b_sb"][0]),
        "conv_w": f(inp["conv_w"][0]), "w_attn_out": f(inp["w_attn_out"][0]), "w_conv_out": f(inp["w_conv_out"][0]),
        "w_out": f(inp["w_out"][0]), "w_ffn_in": f(inp["w_ffn_in"][0]), "w_ffn_out": f(inp["w_ffn_out"][0]),
    }
    maps = []
    for c in range(8):
        b, par = c // 2, c % 2
        m = dict(shared)
        xp = np.asarray(inp["x_prompt"], dtype=np.float32)
        m["x_own"] = np.ascontiguousarray(xp[b, par * 1024:(par + 1) * 1024])
        m["x_oth"] = np.ascontiguousarray(xp[b, (1 - par) * 1024:(2 - par) * 1024])
        m["x_s"] = f(inp["x_sample"][c, 0])
        m["c_p"] = f(inp["c_prompt"][b])
        m["c_s"] = f(inp["c_sample"][c])
        fl = np.zeros((128, 2), np.float32)
        fl[:, 0] = float(par)
        fl[:, 1] = 0.0 if par == 1 else NEG
        m["flags"] = fl
        m["page_tbl"] = np.ascontiguousarray(np.asarray(inp["page_table"], dtype=np.int32)[c:c + 1])
        m["state_conv"] = f(inp["state_conv"][0, c])
        maps.append(m)
    return maps


def assemble(results):
    y_prompt = np.zeros((4, 2048, 2048), np.float32)
    k_prompt = np.zeros((1, 4, 2048, 16, 128), np.float32)
    v_prompt = np.zeros((1, 4, 2048, 16, 128), np.float32)
    conv_prompt = np.zeros((1, 4, 2, 1024), np.float32)
    y_sample = np.zeros((8, 1, 2048), np.float32)
    k_sample = np.zeros((1, 8, 1, 16, 128), np.float32)
    v_sample = np.zeros((1, 8, 1, 16, 128), np.float32)
    conv_sample = np.zeros((1, 8, 2, 1024), np.float32)
    for c in range(8):
        r = results[c]
        b, par = c // 2, c % 2
        sl = slice(par * 1024, (par + 1) * 1024)
        y_prompt[b, sl] = r["y_own"]
        k_prompt[0, b, sl] = r["k_own"].reshape(1024, 16, 128)
        v_prompt[0, b, sl] = r["v_own"].reshape(1024, 16, 128)
        if par == 1:
            conv_prompt[0, b] = r["conv_p"]
        y_sample[c, 0] = r["y_s"]
        k_sample[0, c, 0] = r["k_s"].reshape(16, 128)
        v_sample[0, c, 0] = r["v_s"].reshape(16, 128)
        conv_sample[0, c] = r["conv_s"]
    return (y_prompt, y_sample, k_prompt, v_prompt, conv_prompt, k_sample, v_sample, conv_sample)


_NC_CACHE = {}


def kernel(**inputs):
    if "nc" not in _NC_CACHE:
        _NC_CACHE["nc"] = build_nc()
    nc = _NC_CACHE["nc"]
    maps = make_in_maps(inputs)
    res = run_bass_kernel_spmd(nc, maps, core_ids=list(range(8)))
    return assemble(res.results)
```

```python
import numpy as np
from contextlib import ExitStack
import concourse.bass as bass
import concourse.mybir as mybir
from concourse.bass_utils import run_bass_kernel_spmd

F32 = mybir.dt.float32
BF16 = mybir.dt.bfloat16
I32 = mybir.dt.int32
U8 = mybir.dt.uint8
AF = mybir.ActivationFunctionType
ALU = mybir.AluOpType
AX = mybir.AxisListType
DT_SIZE = {F32: 4, BF16: 2, I32: 4, U8: 1}

D = 2048
KC = 16
H = 16
NT = 1024
G = 512
DC = 1024
DFF = 5632
D_IN = 13312
OFF_Q, OFF_K, OFF_V, OFF_CB, OFF_CC, OFF_CH, OFF_GA, OFF_GB = 0, 2048, 4096, 6144, 7168, 8192, 9216, 11264
NPAGES = 128
NPOOL = 1280
EPS = 1e-6
NEG = -30000.0
QSCALE = 128 ** -0.5


class Tile:
    __slots__ = ("name", "writers", "readers", "excl")

    def __init__(self, name, inherited=(), excl=False):
        self.name = name
        self.writers = list(inherited)
        self.readers = []
        self.excl = excl


class Op:
    __slots__ = ("eng", "fn", "deps", "kind", "semkey", "val", "need_signal", "ndma", "name", "seq")
    _ctr = [0]

    def __init__(self, eng, fn, kind, name=""):
        self.eng = eng
        self.fn = fn
        self.kind = kind
        self.deps = []
        self.semkey = None
        self.val = None
        self.need_signal = False
        self.ndma = 0
        self.name = name
        Op._ctr[0] += 1
        self.seq = Op._ctr[0]


def _compress(ops):
    last = {}
    out = {}
    for o in ops:
        if o.kind == "dma":
            out[id(o)] = o
        elif o.eng not in last or last[o.eng].seq < o.seq:
            last[o.eng] = o
    return list(out.values()) + list(last.values())


class Buf:
    def __init__(self, name, ap, start, end, inherited):
        self.name = name
        self.ap = ap
        self.start = start
        self.end = end
        self.inherited = inherited
        self.tiles = {}

    def t(self, key=0):
        tl = self.tiles.get(key)
        if tl is None:
            tl = Tile(f"{self.name}[{key}]", self.inherited)
            self.tiles[key] = tl
        return tl

    def hazards(self):
        ops = list(self.inherited)
        for tl in self.tiles.values():
            ops += tl.writers + tl.readers
        return _compress(ops)


class Arena:
    def __init__(self, base_ap, size):
        self.base = base_ap
        self.size = size
        self.live = []
        self.dead = []
        self.peak = 0

    BASE = 46080

    def alloc(self, name, free_shape, dtype, align=64, top=False, at=None):
        nbytes = int(np.prod(free_shape)) * DT_SIZE[dtype]
        self.live.sort(key=lambda b: b.start)
        if at is not None:
            start = self.BASE + at * 1024
            for b in self.live:
                assert b.end <= start or b.start >= start + nbytes, (name, b.name, b.start, b.end, start, nbytes)
            assert start + nbytes <= self.size, name
        else:
            gaps = []
            pos = 0
            for b in self.live:
                if b.start - pos >= nbytes:
                    gaps.append((pos, b.start))
                pos = max(pos, (b.end + align - 1) // align * align)
            if self.size - pos >= nbytes:
                gaps.append((pos, self.size))
            if not gaps:
                raise RuntimeError(f"SBUF arena full allocating {name} ({nbytes}B); live="
                                   + ",".join(f"{b.name}:{b.start}-{b.end}" for b in self.live))
            gs, ge = gaps[0]
            start = gs
        end = start + nbytes
        inh = []
        nd = []
        for (s, e, ops) in self.dead:
            if s < end and start < e:
                inh += ops
                if s >= start and e <= end:
                    continue
            nd.append((s, e, ops))
        self.dead = nd
        inh = _compress(inh)
        ap = self.base[:, start:end]
        if dtype != U8:
            ap = ap.bitcast(dtype)
        if len(free_shape) == 2:
            ap = ap.rearrange("p (a b) -> p a b", a=free_shape[0])
        elif len(free_shape) == 3:
            ap = ap.rearrange("p (a b c) -> p a b c", a=free_shape[0], b=free_shape[1])
        b = Buf(name, ap, start, end, inh)
        self.live.append(b)
        self.peak = max(self.peak, end)
        return b

    def free(self, buf):
        self.live.remove(buf)
        self.dead.append((buf.start, buf.end, buf.hazards()))


class Prog:
    ENGS = ("pe", "act", "dve", "pool", "sp")

    def __init__(self, nc):
        self.nc = nc
        self.ops = []
        self.dma_counts = {}
        self.final_waits = []

    def _add(self, op, reads, writes):
        deps = {}
        for t in reads:
            for w in t.writers:
                deps[id(w)] = (w, True)
            if t.excl:
                for r in t.readers:
                    if r.eng != op.eng and id(r) not in deps:
                        deps[id(r)] = (r, False)
        for t in writes:
            for w in t.writers:
                if id(w) not in deps:
                    deps[id(w)] = (w, False)
            for r in t.readers:
                if id(r) not in deps:
                    deps[id(r)] = (r, False)
        op.deps = [d for d in deps.values() if d[0] is not op]
        for t in reads:
            t.readers.append(op)
            if len(t.readers) > 24:
                t.readers = _compress(t.readers)
        for t in writes:
            t.writers = [op]
            t.readers = []
        self.ops.append(op)
        return op

    def add(self, eng, fn, reads=(), writes=(), name=""):
        return self._add(Op(eng, fn, "c", name), reads, writes)

    def dma(self, eng, fn, semkey, ndma, reads=(), writes=(), name="", final=False):
        op = Op(eng, fn, "dma", name)
        op.semkey = semkey
        op.ndma = ndma
        c = self.dma_counts.get(semkey, 0) + ndma
        self.dma_counts[semkey] = c
        op.val = 16 * c
        self._add(op, reads, writes)
        if final:
            self.final_waits.append(op)
        return op

    @staticmethod
    def _needs_sync(cons, prod, is_raw):
        if prod.kind == "dma" or cons.kind == "dma":
            return True
        if prod.eng != cons.eng:
            return True
        if cons.eng == "pe":
            return False
        return True

    def emit(self, stack):
        nc = self.nc
        for op in self.ops:
            for (d, is_raw) in op.deps:
                if self._needs_sync(op, d, is_raw):
                    d.need_signal = True
        counters = {e: 0 for e in self.ENGS}
        for op in self.ops:
            if op.kind == "c" and op.need_signal:
                counters[op.eng] += 1
                op.val = counters[op.eng]
                op.semkey = "eng_" + op.eng
        sems = {}
        for e in self.ENGS:
            sems["eng_" + e] = stack.enter_context(nc.semaphore("s_eng_" + e))
        for k in self.dma_counts:
            sems[k] = stack.enter_context(nc.semaphore("s_" + k))
        per_eng = {e: [] for e in self.ENGS}
        for op in self.ops:
            per_eng[op.eng].append(op)
        block = stack.enter_context(nc.Block())

        def make(ename, ops):
            def body(eng):
                waited = {}
                for op in ops:
                    for (d, is_raw) in op.deps:
                        if not self._needs_sync(op, d, is_raw):
                            continue
                        if waited.get(d.semkey, 0) >= d.val:
                            continue
                        eng.wait_ge(sems[d.semkey], d.val)
                        waited[d.semkey] = d.val
                    if op.kind == "c":
                        ins = op.fn(eng)
                        if op.need_signal:
                            ins.then_inc(sems[op.semkey], 1)
                    else:
                        lst = op.fn(eng)
                        assert len(lst) == op.ndma, (op.name, len(lst), op.ndma)
                        for ins in lst:
                            ins.then_inc(sems[op.semkey], 16)
                if ename == "sp":
                    for op in self.final_waits:
                        if waited.get(op.semkey, 0) >= op.val:
                            continue
                        eng.wait_ge(sems[op.semkey], op.val)
                        waited[op.semkey] = op.val
            return body

        block.tensor(make("pe", per_eng["pe"]))
        block.scalar(make("act", per_eng["act"]))
        block.vector(make("dve", per_eng["dve"]))
        block.gpsimd(make("pool", per_eng["pool"]))
        block.sync(make("sp", per_eng["sp"]))
        return counters


class WQ:
    NS = 2
    SLOT_ELEMS = 8192

    def __init__(self, P, ar, plan):
        self.P = P
        self.plan = plan
        self.rec = []
        self.i = 0
        self.issued = 0
        self.done_n = 0
        self.slots = [ar.alloc(f"wslot{s}", [self.SLOT_ELEMS], BF16) for s in range(self.NS)]

    def _view(self, s, nk, ncols):
        return self.slots[s].ap[:, 0:nk * ncols].rearrange("p (k n) -> p k n", k=nk)

    def _issue(self, j, desc):
        (w, r0, nr, c0, ncols) = desc
        s = j % self.NS
        nk = nr // 128
        view = self._view(s, nk, ncols)
        src = w[r0:r0 + nr, c0:c0 + ncols].rearrange("(k p) n -> p k n", p=128)
        self.P.dma("pool", lambda e, view=view, src=src: [e.dma_start(out=view, in_=src)],
                   f"wslot{s}", 1, writes=[self.slots[s].t()], name=f"wload{j}")

    def _pump(self):
        if self.plan is None:
            return
        lim = min(self.done_n + self.NS, len(self.plan))
        while self.issued < lim:
            self._issue(self.issued, self.plan[self.issued])
            self.issued += 1

    def next(self, w, r0, nr, c0, ncols):
        desc = (w, r0, nr, c0, ncols)
        idx = self.i
        self.i += 1
        self.rec.append(desc)
        assert self.done_n >= idx - self.NS + 1, "too many live weight tiles"
        if self.plan is None:
            self._issue(idx, desc)
        else:
            pd = self.plan[idx]
            assert pd[1:] == desc[1:], (idx, pd[1:], desc[1:])
            self._pump()
            assert self.issued > idx
        s = idx % self.NS
        return self._view(s, nr // 128, ncols), self.slots[s].t()

    def done(self):
        self.done_n += 1
        self._pump()


def build_core_program(nc, T, plan, debug=None):
    st = ExitStack()
    with st:
        SZ = 206 * 1024
        arena_t = st.enter_context(nc.sbuf_tensor("arena", [128, SZ], U8))
        ar = Arena(arena_t[:, :], SZ)
        banks = [st.enter_context(nc.psum_tensor(f"bank{i}", [128, 512], F32)) for i in range(8)]
        bank_t = [Tile(f"bank{i}", excl=True) for i in range(8)]
        P = Prog(nc)
        wq = WQ(P, ar, plan)
        NCD = dict(allow_slow_non_contiguous=True)

        class RR:
            def __init__(self, ids):
                self.ids = ids
                self.i = 0

            def next(self):
                b = self.ids[self.i % len(self.ids)]
                self.i += 1
                return b
        acc_rr = RR([0, 1, 2, 3, 4, 5])
        SB = 7

        def bk(i):
            return banks[i][:, :]

        ones_f = ar.alloc("ones_f", [128], F32)
        zeros_b = ar.alloc("zeros_b", [512], BF16)
        ones_b = ar.alloc("ones_b", [128], BF16)
        ident_f = ar.alloc("ident_f", [128], F32)
        ident_b = ar.alloc("ident_b", [128], BF16)
        tri_b = ar.alloc("tri_b", [128], BF16)
        tri_f = ar.alloc("tri_f", [128], F32)
        masks = ar.alloc("masks", [4, 512], BF16)
        flags = ar.alloc("flags", [2], F32)
        bsb = ar.alloc("bsb", [16], F32)
        bsb_o = ar.alloc("bsb_o", [16], F32)
        gm2 = ar.alloc("gm2", [16, 2], F32)
        gf2 = ar.alloc("gf2", [16, 2], F32)
        gfin = ar.alloc("gfin", [16], F32)
        bm2 = ar.alloc("bm2", [96, 2], F32)
        cw = ar.alloc("cw", [8, 3], F32)
        stc = ar.alloc("stc", [8, 2], F32)
        modT = ar.alloc("modT", [96, 2], F32)
        A1 = ar.alloc("A1", [16, 2], F32)
        A2 = ar.alloc("A2", [16, 2], F32)
        cT = ar.alloc("cT", [16, 2], F32)
        scT = ar.alloc("scT", [16, 2], BF16)
        xs = ar.alloc("xs", [16], F32)
        hs = ar.alloc("hs", [16], BF16)
        qs = ar.alloc("qs", [16], F32)
        ks = ar.alloc("ks", [16], F32)
        vs = ar.alloc("vs", [16], F32)
        os_b = ar.alloc("os_b", [16], BF16)
        ocs_b = ar.alloc("ocs_b", [8], BF16)
        mgs_b = ar.alloc("mgs_b", [16], BF16)
        gs_b = ar.alloc("gs_b", [44], BF16)
        tiny = ar.alloc("tiny", [64], F32)
        ptb = ar.alloc("ptb", [128], I32)
        pidx = ar.alloc("pidx", [128], I32)
        iof = ar.alloc("iof", [1], F32)
        cpst = ar.alloc("cpst", [8, 2], F32)
        csst = ar.alloc("csst", [8, 2], F32)
        sm = ar.alloc("sm", [16], F32)
        smc = ar.alloc("smc", [4, 16], F32)

        P.add("pool", lambda e: e.memset(ones_f.ap, 1.0), writes=[ones_f.t()])
        P.add("pool", lambda e: e.memset(ones_b.ap, 1.0), writes=[ones_b.t()])
        P.add("pool", lambda e: e.memset(zeros_b.ap, 0.0), writes=[zeros_b.t()])
        P.add("pool", lambda e: e.affine_select(out=ident_f.ap, in_=ones_f.ap, pattern=[[-1, 128]],
                                                compare_op=ALU.is_equal, fill=0.0, base=0, channel_multiplier=1),
              reads=[ones_f.t()], writes=[ident_f.t()])
        P.add("pool", lambda e: e.affine_select(out=ident_b.ap, in_=ones_b.ap, pattern=[[-1, 128]],
                                                compare_op=ALU.is_equal, fill=0.0, base=0, channel_multiplier=1),
              reads=[ones_b.t()], writes=[ident_b.t()])
        P.add("pool", lambda e: e.affine_select(out=tri_b.ap, in_=ones_b.ap, pattern=[[-1, 128]],
                                                compare_op=ALU.is_ge, fill=0.0, base=0, channel_multiplier=1),
              reads=[ones_b.t()], writes=[tri_b.t()])
        P.add("pool", lambda e: e.affine_select(out=tri_f.ap, in_=ones_f.ap, pattern=[[-1, 128]],
                                                compare_op=ALU.is_ge, fill=0.0, base=0, channel_multiplier=1),
              reads=[ones_f.t()], writes=[tri_f.t()])
        for m in range(4):
            P.add("pool", lambda e, m=m: e.affine_select(out=masks.ap[:, m, :], in_=zeros_b.ap, pattern=[[1, 512]],
                                                         compare_op=ALU.is_gt, fill=NEG, base=-128 * m,
                                                         channel_multiplier=-1),
                  reads=[zeros_b.t()], writes=[masks.t(m)])

        def fm_vec(ap1d):
            return ap1d.rearrange("(c p) -> p c", p=128)

        def ld_consts(e):
            l = []
            l.append(e.dma_start(out=flags.ap, in_=T["flags"]))
            l.append(e.dma_start(out=bsb.ap, in_=T["b_sb"].partition_broadcast(128)))
            l.append(e.dma_start(out=gm2.ap[:, :, 0], in_=fm_vec(T["g_mix"]), **NCD))
            l.append(e.dma_start(out=gm2.ap[:, :, 1], in_=fm_vec(T["g_mix"]), **NCD))
            l.append(e.dma_start(out=gf2.ap[:, :, 0], in_=fm_vec(T["g_ffn"]), **NCD))
            l.append(e.dma_start(out=gf2.ap[:, :, 1], in_=fm_vec(T["g_ffn"]), **NCD))
            l.append(e.dma_start(out=gfin.ap, in_=fm_vec(T["g_final"]), **NCD))
            l.append(e.dma_start(out=bm2.ap[:, :, 0], in_=fm_vec(T["b_mod"]), **NCD))
            l.append(e.dma_start(out=bm2.ap[:, :, 1], in_=fm_vec(T["b_mod"]), **NCD))
            for jt in range(3):
                l.append(e.dma_start(out=cw.ap[:, :, jt], in_=fm_vec(T["conv_w"][jt, :]), **NCD))
            for jt in range(2):
                l.append(e.dma_start(out=stc.ap[:, :, jt], in_=fm_vec(T["state_conv"][jt, :]), **NCD))
            l.append(e.dma_start(out=cT.ap[:, :, 0], in_=fm_vec(T["c_p"]), **NCD))
            l.append(e.dma_start(out=cT.ap[:, :, 1], in_=fm_vec(T["c_s"]), **NCD))
            l.append(e.dma_start(out=xs.ap, in_=fm_vec(T["x_s"]), **NCD))
            l.append(e.dma_start(out=ptb.ap, in_=T["page_tbl"].partition_broadcast(128)))
            return l
        cts = [flags.t(), bsb.t(), gm2.t(), gf2.t(), gfin.t(), bm2.t(), cw.t(), stc.t(), cT.t(), xs.t(), ptb.t()]
        P.dma("sp", ld_consts, "consts", 18, writes=cts, name="ld_consts")
        P.add("pool", lambda e: e.iota(iof.ap, [[0, 1]], base=0, channel_multiplier=1, allow_small_or_imprecise_dtypes=True),
              writes=[iof.t()])
        P.add("dve", lambda e: e.tensor_scalar(out=pidx.ap, in0=ptb.ap, scalar1=128.0, scalar2=iof.ap[:, 0:1], op0=ALU.mult, op1=ALU.add),
              reads=[ptb.t(), iof.t()], writes=[pidx.t()])
        P.add("dve", lambda e: e.tensor_scalar(out=bsb_o.ap, in0=bsb.ap, scalar1=flags.ap[:, 1:2], scalar2=None,
                                               op0=ALU.add),
              reads=[bsb.t(), flags.t()], writes=[bsb_o.t()])
        P.add("act", lambda e: e.activation(out=scT.ap, in_=cT.ap, func=AF.Silu), reads=[cT.t()], writes=[scT.t()])

        def fm_group(wv, wt, col, nk, movs, outs, name=""):
            reads = [wt]
            for (_, tl) in movs:
                reads += tl
            writes = [bank_t[b] for (_, b) in outs]

            def fn(e):
                ins = None
                for k in range(nk):
                    lhsT = wv[:, k, col * 128:(col + 1) * 128]
                    for (apf, _), (oap, _) in zip(movs, outs):
                        ins = e.matmul(oap, lhsT=lhsT, rhs=apf(k), start=(k == 0), stop=(k == nk - 1))
                return ins
            P.add("pe", fn, reads=reads, writes=writes, name=name)

        mov_c = (lambda k: scT.ap[:, k, :], [scT.t()])
        for tix in range(24):
            wv, wt = wq.next(T["w_mod"], 0, 2048, tix * 512, 512)
            for j in range(4):
                fm_group(wv, wt, j, 16, [mov_c], [(banks[SB][:, 2 * j:2 * j + 2], SB)], name="mod")
            P.add("dve", lambda e, tix=tix: e.tensor_tensor(
                out=modT.ap[:, tix * 4:tix * 4 + 4, :],
                in0=banks[SB][:, 0:8].rearrange("p (a b) -> p a b", b=2),
                in1=bm2.ap[:, tix * 4:tix * 4 + 4, :], op=ALU.add),
                reads=[bank_t[SB], bm2.t()], writes=[modT.t(tix)])
            wq.done()
        mod_all = [modT.t(i) for i in range(24)]
        P.add("dve", lambda e: e.scalar_tensor_tensor(out=A1.ap, in0=modT.ap[:, 16:32, :], scalar=1.0, in1=gm2.ap,
                                                      op0=ALU.add, op1=ALU.mult),
              reads=mod_all + [gm2.t()], writes=[A1.t()])
        P.add("dve", lambda e: e.scalar_tensor_tensor(out=A2.ap, in0=modT.ap[:, 64:80, :], scalar=1.0, in1=gf2.ap,
                                                      op0=ALU.add, op1=ALU.mult),
              reads=mod_all + [gf2.t()], writes=[A2.t()])
        MODT = Tile("modT_done")
        P.add("dve", lambda e: e.tensor_copy(out=tiny.ap[:, 0:1], in_=modT.ap[:, 0, 0:1]), reads=mod_all + [A1.t(), A2.t()],
              writes=[MODT, tiny.t()])

        def B1(c, t):
            return modT.ap[:, c, t:t + 1]

        def GATE1(c, t):
            return modT.ap[:, 32 + c, t:t + 1]

        def B2(c, t):
            return modT.ap[:, 48 + c, t:t + 1]

        def GATE2(c, t):
            return modT.ap[:, 80 + c, t:t + 1]

        def sample_norm(Aap, Bap, out_buf, name):
            sq = tiny.ap[:, 0:16]
            ssum = tiny.ap[:, 16:17]
            rs = tiny.ap[:, 17:18]
            tmp = tiny.ap[:, 32:48]
            P.add("dve", lambda e: e.tensor_tensor(out=sq, in0=xs.ap, in1=xs.ap, op=ALU.mult),
                  reads=[xs.t()], writes=[tiny.t()])
            P.add("dve", lambda e: e.tensor_reduce(out=ssum, in_=sq, axis=AX.X, op=ALU.add),
                  reads=[tiny.t()], writes=[tiny.t()])
            P.add("pe", lambda e: e.matmul(banks[SB][:, 0:1], lhsT=ones_f.ap, rhs=ssum, start=True, stop=True),
                  reads=[tiny.t(), ones_f.t()], writes=[bank_t[SB]])
            P.add("dve", lambda e: e.tensor_scalar(out=rs, in0=banks[SB][:, 0:1], scalar1=1.0 / D, scalar2=EPS,
                                                   op0=ALU.mult, op1=ALU.add),
                  reads=[bank_t[SB]], writes=[tiny.t()])
            P.add("act", lambda e: e.sqrt(out=rs, in_=rs), reads=[tiny.t()], writes=[tiny.t()])
            P.add("dve", lambda e: e.reciprocal(out=rs, in_=rs), reads=[tiny.t()], writes=[tiny.t()])
            P.add("dve", lambda e: e.scalar_tensor_tensor(out=tmp, in0=xs.ap, scalar=rs, in1=Aap,
                                                          op0=ALU.mult, op1=ALU.mult),
                  reads=[tiny.t(), xs.t(), MODT], writes=[tiny.t()])
            if Bap is not None:
                P.add("dve", lambda e: e.tensor_tensor(out=out_buf.ap, in0=tmp, in1=Bap, op=ALU.add),
                      reads=[tiny.t(), MODT], writes=[out_buf.t()])
            else:
                P.add("dve", lambda e: e.tensor_copy(out=out_buf.ap, in_=tmp), reads=[tiny.t()], writes=[out_buf.t()])

        STATS = 6

        def group_norm_stats(src_fn, src_tiles_fn, sqb, rstd, tag):
            for k in range(KC):
                sb_ = sqb[k % 2]
                P.add("act", lambda e, k=k, sb_=sb_: e.activation(out=sb_.ap, in_=src_fn(k), func=AF.Square),
                      reads=src_tiles_fn(k), writes=[sb_.t()])
                P.add("pe", lambda e, k=k, sb_=sb_: e.matmul(bk(STATS), lhsT=ones_b.ap, rhs=sb_.ap,
                                                             start=(k == 0), stop=(k == KC - 1)),
                      reads=[sb_.t(), ones_b.t()], writes=[bank_t[STATS]])
            P.add("dve", lambda e: e.tensor_scalar(out=rstd.ap, in0=bk(STATS), scalar1=1.0 / D, scalar2=EPS,
                                                   op0=ALU.mult, op1=ALU.add),
                  reads=[bank_t[STATS]], writes=[rstd.t()])
            P.add("act", lambda e: e.sqrt(out=rstd.ap, in_=rstd.ap), reads=[rstd.t()], writes=[rstd.t()])
            P.add("dve", lambda e: e.reciprocal(out=rstd.ap, in_=rstd.ap), reads=[rstd.t()], writes=[rstd.t()])

        hT = ar.alloc("hT", [16, 1024], BF16, at=129)
        hTo = ar.alloc("hTo", [16, 1024], BF16, at=97)
        xin = [ar.alloc("xin0", [4, 2048], F32)]
        xTg = ar.alloc("xTg", [16, 512], F32)
        sqb = [ar.alloc(f"sqb{i}", [512], BF16) for i in range(2)]
        rstd = ar.alloc("rstd", [512], F32)
        ntmp = [ar.alloc(f"ntmp{i}", [512], F32) for i in range(2)]

        def load_x(src, g, xb):
            srcv = src[g * 512:(g + 1) * 512, :].rearrange("(b p) f -> p b f", p=128)
            P.dma("sp", lambda e: [e.dma_start(out=xb.ap, in_=srcv)], "xin0", 1,
                  writes=[xb.t()], name="load_x")

        def transpose_group(xb, dst_fn, dst_tile_fn):
            for k in range(KC):
                b = acc_rr.next()

                def tr(e, k=k, b=b):
                    ins = None
                    for j in range(4):
                        ins = e.transpose(out=banks[b][:, j * 128:(j + 1) * 128],
                                          in_=xb.ap[:, j, k * 128:(k + 1) * 128], identity=ident_f.ap)
                    return ins
                P.add("pe", tr, reads=[xb.t(), ident_f.t()], writes=[bank_t[b]], name="xtr")
                P.add("dve", lambda e, k=k, b=b: e.tensor_copy(out=dst_fn(k), in_=bk(b)),
                      reads=[bank_t[b]], writes=[dst_tile_fn(k)])

        def norm_apply(src_fn, src_tile_fn, dst_buf, g, Aap_fn, Bap_fn, rstd, ntmp):
            for k in range(KC):
                tb_ = ntmp[k % 2]
                P.add("dve", lambda e, k=k, tb_=tb_: e.tensor_tensor(out=tb_.ap, in0=src_fn(k), in1=rstd.ap, op=ALU.mult),
                      reads=[src_tile_fn(k), rstd.t()], writes=[tb_.t()])
                P.add("act", lambda e, k=k, tb_=tb_: e.activation(out=dst_buf.ap[:, k, g * 512:(g + 1) * 512], in_=tb_.ap,
                                                                  func=AF.Identity, bias=Bap_fn(k), scale=Aap_fn(k)),
                      reads=[tb_.t(), MODT], writes=[dst_buf.t((g, k))])

        groups = [(T["x_own"], 0, hT), (T["x_own"], 1, hT), (T["x_oth"], 0, hTo), (T["x_oth"], 1, hTo)]
        for gi, (src, g, dst) in enumerate(groups):
            load_x(src, g, xin[0])
            transpose_group(xin[0], lambda k: xTg.ap[:, k, :], lambda k: xTg.t(k))
            group_norm_stats(lambda k: xTg.ap[:, k, :], lambda k: [xTg.t(k)], sqb, rstd, f"n1_{gi}")
            norm_apply(lambda k: xTg.ap[:, k, :], lambda k: xTg.t(k), dst, g,
                       lambda k: A1.ap[:, k, 0:1], lambda k: B1(k, 0), rstd, ntmp)
        sample_norm(A1.ap[:, :, 1], modT.ap[:, 0:16, 1], hs, "sn1")
        ar.free(xTg)
        for b_ in xin + sqb + ntmp + [rstd]:
            ar.free(b_)

        mov_own = [(lambda k, g=g: hT.ap[:, k, g * 512:(g + 1) * 512], [hT.t((g, k)) for k in range(KC)]) for g in range(2)]
        mov_oth = [(lambda k, g=g: hTo.ap[:, k, g * 512:(g + 1) * 512], [hTo.t((g, k)) for k in range(KC)]) for g in range(2)]
        mov_hs = (lambda k: hs.ap[:, k:k + 1], [hs.t()])

        HP = 2
        NHG = H // HP
        WC = HP * 128
        oT = ar.alloc("oT", [16, 1024], BF16, at=65)
        qT = ar.alloc("qT", [HP, 1024], BF16)
        kT = ar.alloc("kT", [HP, 2048], BF16)
        vv = ar.alloc("vv", [16, WC], BF16)
        kst = [ar.alloc(f"kst{i}", [2, WC], F32) for i in range(2)]
        vst = [ar.alloc(f"vst{i}", [2, WC], F32) for i in range(2)]
        Eb = [[ar.alloc(f"E{g}{i}", [512], F32) for i in range(2)] for g in range(2)]
        LKb = [[ar.alloc(f"LK{g}{i}", [512], BF16) for i in range(2)] for g in range(2)]
        XSb = [[ar.alloc(f"XS{g}{i}", [512], BF16) for i in range(2)] for g in range(2)]
        Wb = [[ar.alloc(f"W{g}{i}", [512], BF16) for i in range(2)] for g in range(2)]
        LKacc = [ar.alloc(f"LKacc{g}", [512], F32) for g in range(2)]
        LKaccb = [[ar.alloc(f"LKaccb{g}{i}", [512], BF16) for i in range(2)] for g in range(2)]
        BA = [0, 1]
        BB = [2, 3]
        BO = [4, 5]

        def evac_sample(dst_ap, dst_tile, ncol, scale=None):
            if scale is None:
                P.add("dve", lambda e: e.tensor_copy(out=dst_ap, in_=banks[SB][:, 0:ncol]),
                      reads=[bank_t[SB]], writes=[dst_tile])
            else:
                P.add("dve", lambda e: e.tensor_scalar(out=dst_ap, in0=banks[SB][:, 0:ncol], scalar1=scale, scalar2=None,
                                                       op0=ALU.mult),
                      reads=[bank_t[SB]], writes=[dst_tile])

        def attn_stage1(c, j, h):
            (g, i, (kb, mi, bias), first, last) = c
            A = BA[g]
            E = Eb[g][i % 2]
            LK = LKb[g][i % 2]

            def zmm(e):
                ins = e.matmul(bk(A), lhsT=kT.ap[:, j, kb * 128:(kb + 1) * 128], rhs=qT.ap[:, j, g * 512:(g + 1) * 512],
                               start=True, stop=(mi is None))
                if mi is not None:
                    ins = e.matmul(bk(A), lhsT=ident_b.ap, rhs=masks.ap[:, mi, :], start=False, stop=True)
                return ins
            P.add("pe", zmm, reads=[kT.t((j, kb // 4)), qT.t((j, g)), ident_b.t()] + [masks.t(m) for m in range(4)],
                  writes=[bank_t[A]], name="z")
            P.add("act", lambda e: e.activation(out=E.ap, in_=bk(A), func=AF.Exp, bias=bias.ap[:, h:h + 1], scale=1.0),
                  reads=[bank_t[A], bias.t()], writes=[E.t()])
            P.add("act", lambda e: e.activation(out=LK.ap, in_=E.ap, func=AF.Ln, bias=1.0, scale=1.0),
                  reads=[E.t()], writes=[LK.t()])

        def attn_stage2(c, j, h):
            (g, i, (kb, mi, bias), first, last) = c
            Bk = BB[g]
            LK = LKb[g][i % 2]
            XS = XSb[g][i % 2]
            accb_prev = LKaccb[g][(i + 1) % 2]
            accb_new = LKaccb[g][i % 2]
            acc = LKacc[g]

            def smm(e):
                ins = e.matmul(bk(Bk), lhsT=tri_b.ap, rhs=LK.ap, start=True, stop=first)
                if not first:
                    ins = e.matmul(bk(Bk), lhsT=ones_b.ap, rhs=accb_prev.ap, start=False, stop=True)
                return ins
            P.add("pe", smm, reads=[LK.t(), tri_b.t(), ones_b.t()] + ([] if first else [accb_prev.t()]),
                  writes=[bank_t[Bk]], name="sfx")
            if not last:
                if first:
                    P.add("dve", lambda e: e.tensor_copy(out=acc.ap, in_=LK.ap), reads=[LK.t()], writes=[acc.t()])
                else:
                    P.add("dve", lambda e: e.tensor_tensor(out=acc.ap, in0=acc.ap, in1=LK.ap, op=ALU.add),
                          reads=[LK.t(), acc.t()], writes=[acc.t()])
                P.add("dve", lambda e: e.tensor_copy(out=accb_new.ap, in_=acc.ap),
                      reads=[acc.t()], writes=[accb_new.t()])
            P.add("act", lambda e: e.activation(out=XS.ap, in_=bk(Bk), func=AF.Exp, scale=-1.0),
                  reads=[bank_t[Bk]], writes=[XS.t()])

        def attn_stage3(c, j, h):
            (g, i, (kb, mi, bias), first, last) = c
            E = Eb[g][i % 2]
            XS = XSb[g][i % 2]
            W = Wb[g][i % 2]
            O = BO[g]
            P.add("dve", lambda e: e.tensor_tensor(out=W.ap, in0=E.ap, in1=XS.ap, op=ALU.mult),
                  reads=[E.t(), XS.t()], writes=[W.t()])
            P.add("pe", lambda e: e.matmul(bk(O), lhsT=vv.ap[:, kb, j * 128:(j + 1) * 128], rhs=W.ap,
                                           start=first, stop=last),
                  reads=[vv.t(kb), W.t()], writes=[bank_t[O]], name="av")
            if last:
                P.add("act", lambda e: e.copy(out=oT.ap[:, h, g * 512:(g + 1) * 512], in_=bk(O)),
                      reads=[bank_t[O]], writes=[oT.t((g, h))])

        for hq in range(NHG):
            wvq, wtq = wq.next(T["w_in"], 0, 2048, OFF_Q + hq * WC, WC)
            for j in range(HP):
                b0, b1 = acc_rr.next(), acc_rr.next()
                fm_group(wvq, wtq, j, 16, mov_own + [mov_hs], [(bk(b0), b0), (bk(b1), b1), (banks[SB][:, j:j + 1], SB)], "q")
                for g, b in ((0, b0), (1, b1)):
                    P.add("act", lambda e, j=j, g=g, b=b: e.activation(out=qT.ap[:, j, g * 512:(g + 1) * 512], in_=bk(b),
                                                                       func=AF.Identity, scale=QSCALE),
                          reads=[bank_t[b]], writes=[qT.t((j, g))])
            evac_sample(qs.ap[:, hq * HP:hq * HP + HP], qs.t(hq), HP, scale=QSCALE)
            wq.done()
            wvk, wtk = wq.next(T["w_in"], 0, 2048, OFF_K + hq * WC, WC)
            for j in range(HP):
                b0, b1 = acc_rr.next(), acc_rr.next()
                fm_group(wvk, wtk, j, 16, mov_own + [mov_hs], [(bk(b0), b0), (bk(b1), b1), (banks[SB][:, j:j + 1], SB)], "k")
                for g, b in ((0, b0), (1, b1)):
                    P.add("act", lambda e, j=j, g=g, b=b: e.copy(out=kT.ap[:, j, g * 512:(g + 1) * 512], in_=bk(b)),
                          reads=[bank_t[b]], writes=[kT.t((j, g))])
                b2, b3 = acc_rr.next(), acc_rr.next()
                fm_group(wvk, wtk, j, 16, mov_oth, [(bk(b2), b2), (bk(b3), b3)], "ko")
                for g, b in ((0, b2), (1, b3)):
                    P.add("dve", lambda e, j=j, g=g, b=b: e.tensor_copy(out=kT.ap[:, j, 1024 + g * 512:1024 + (g + 1) * 512], in_=bk(b)),
                          reads=[bank_t[b]], writes=[kT.t((j, 2 + g))])
            evac_sample(ks.ap[:, hq * HP:hq * HP + HP], ks.t(hq), HP)
            wq.done()
            wvv, wtv = wq.next(T["w_in"], 0, 2048, OFF_V + hq * WC, WC)
            for tb in range(8):
                b = acc_rr.next()
                pb = banks[b][:, :].bitcast(BF16)
                ks_ = kst[(tb // 2) % 2]

                def ktr(e, tb=tb, pb=pb):
                    ins = None
                    for j in range(HP):
                        ins = e.transpose(out=pb[:, j * 128:(j + 1) * 128], in_=kT.ap[:, j, tb * 128:(tb + 1) * 128],
                                          identity=ident_b.ap)
                    return ins
                P.add("pe", ktr, reads=[kT.t((j, tb // 4)) for j in range(HP)] + [ident_b.t()], writes=[bank_t[b]], name="ktr")
                P.add("dve", lambda e, tb=tb, pb=pb, ks_=ks_: e.tensor_copy(out=ks_.ap[:, tb % 2, :], in_=pb[:, 0:WC]),
                      reads=[bank_t[b]], writes=[ks_.t(tb % 2)])
                if tb % 2 == 1:
                    kdst = T["k_own"][(tb - 1) * 128:(tb + 1) * 128, hq * WC:(hq + 1) * WC].rearrange("(b p) n -> p b n", p=128)
                    P.dma("sp", lambda e, kdst=kdst, ks_=ks_: [e.dma_start(out=kdst, in_=ks_.ap)], f"kst{(tb // 2) % 2}", 1,
                          reads=[ks_.t(0), ks_.t(1)], final=True, name="kout")
            for tb in range(16):
                b = acc_rr.next()
                src = hT if tb < 8 else hTo
                tbl = tb % 8

                def vmm(e, b=b, src=src, tbl=tbl, wvv=wvv):
                    ins = None
                    for k in range(KC):
                        ins = e.matmul(banks[b][:, 0:WC], lhsT=src.ap[:, k, tbl * 128:(tbl + 1) * 128], rhs=wvv[:, k, :],
                                       start=(k == 0), stop=(k == KC - 1))
                    return ins
                P.add("pe", vmm, reads=[wtv] + [src.t((tbl // 4, k)) for k in range(KC)], writes=[bank_t[b]], name="v")
                if tb < 8:
                    vs_ = vst[(tb // 2) % 2]
                    P.add("dve", lambda e, b=b, tb=tb, vs_=vs_: e.tensor_copy(out=vs_.ap[:, tb % 2, :], in_=banks[b][:, 0:WC]),
                          reads=[bank_t[b]], writes=[vs_.t(tb % 2)])
                    P.add("act", lambda e, tb=tb, vs_=vs_: e.copy(out=vv.ap[:, tb, :], in_=vs_.ap[:, tb % 2, :]),
                          reads=[vs_.t(tb % 2)], writes=[vv.t(tb)])
                    if tb % 2 == 1:
                        vdst = T["v_own"][(tb - 1) * 128:(tb + 1) * 128, hq * WC:(hq + 1) * WC].rearrange("(b p) n -> p b n", p=128)
                        P.dma("sp", lambda e, vdst=vdst, vs_=vs_: [e.dma_start(out=vdst, in_=vs_.ap)], f"vst{(tb // 2) % 2}", 1,
                              reads=[vs_.t(0), vs_.t(1)], final=True, name="vout")
                else:
                    P.add("act", lambda e, b=b, tb=tb: e.copy(out=vv.ap[:, tb, :], in_=banks[b][:, 0:WC]),
                          reads=[bank_t[b]], writes=[vv.t(tb)])
            for j in range(HP):
                fm_group(wvv, wtv, j, 16, [mov_hs], [(banks[SB][:, j:j + 1], SB)], "vs")
            evac_sample(vs.ap[:, hq * HP:hq * HP + HP], vs.t(hq), HP)
            wq.done()

            for j in range(HP):
                h = hq * HP + j
                streams = []
                for g in range(2):
                    cl = []
                    for kb in range(4 * g + 3, 4 * g - 1, -1):
                        cl.append((kb, kb - 4 * g, bsb))
                    if g == 1:
                        for kb in range(3, -1, -1):
                            cl.append((kb, None, bsb))
                    for kb in range(7, -1, -1):
                        cl.append((8 + kb, None, bsb_o))
                    streams.append(cl)
                seq = []
                for i in range(max(len(streams[0]), len(streams[1]))):
                    for g in range(2):
                        if i < len(streams[g]):
                            seq.append((g, i, streams[g][i], i == 0, i == len(streams[g]) - 1))
                n = len(seq)
                for it in range(n + 2):
                    if it < n:
                        attn_stage1(seq[it], j, h)
                    if 0 <= it - 1 < n:
                        attn_stage2(seq[it - 1], j, h)
                    if 0 <= it - 2 < n:
                        attn_stage3(seq[it - 2], j, h)

        for lst in (Eb, LKb, XSb, Wb, LKaccb):
            for l2 in lst:
                for b_ in l2:
                    ar.free(b_)
        for b_ in LKacc + [qT, kT, vv] + kst + vst:
            ar.free(b_)

        P.dma("sp", lambda e: [e.dma_start(out=T["k_s"].rearrange("(h p) -> p h", p=128), in_=ks.ap, **NCD),
                               e.dma_start(out=T["v_s"].rearrange("(h p) -> p h", p=128), in_=vs.ap, **NCD)],
              "kvs", 2, reads=[ks.t(i) for i in range(8)] + [vs.t(i) for i in range(8)], final=True)

        qbf = ar.alloc("qbf", [2048], F32)
        qb = ar.alloc("qb", [2048], BF16)
        zs = ar.alloc("zs", [128, 16], F32)
        NKS = 4
        kpg = [ar.alloc(f"kpg{i}", [2048], BF16) for i in range(NKS)]
        ktmp2 = [ar.alloc(f"ktmp{i}", [2048], F32) for i in range(2)]
        QS_T = Tile("qscr")
        P.dma("sp", lambda e: [e.dma_start(out=T["q_scr"].rearrange("(h p) -> p h", p=128), in_=qs.ap, **NCD)],
              "qscr", 1, reads=[qs.t(i) for i in range(8)], writes=[QS_T], name="qscr_w")
        P.dma("sp", lambda e: [e.dma_start(out=qbf.ap, in_=T["q_scr"].partition_broadcast(128))], "qb", 1,
              reads=[QS_T], writes=[qbf.t()], name="qscr_r")
        P.add("dve", lambda e: e.tensor_copy(out=qb.ap, in_=qbf.ap), reads=[qbf.t()], writes=[qb.t()])
        krows = T["cache_k"].rearrange("n p f -> (n p) f")
        vrows = T["cache_v"].rearrange("n p f -> (n p) f")

        for pg in range(NPAGES):
            slot = pg % NKS
            P.dma("pool", lambda e, pg=pg, slot=slot: [e.indirect_dma_start(
                out=kpg[slot].ap, out_offset=None, in_=krows,
                in_offset=bass.IndirectOffsetOnAxis(ap=pidx.ap[:, pg:pg + 1], axis=0))],
                f"kpg{slot}", 1, reads=[pidx.t()], writes=[kpg[slot].t()], name="kpage")
            kt_ = ktmp2[pg % 2]
            P.add("dve", lambda e, slot=slot, kt_=kt_: e.tensor_tensor(out=kt_.ap, in0=kpg[slot].ap, in1=qb.ap, op=ALU.mult),
                  reads=[kpg[slot].t(), qb.t()], writes=[kt_.t()])
            P.add("dve", lambda e, pg=pg, kt_=kt_: e.tensor_reduce(out=zs.ap[:, pg, :], in_=kt_.ap.rearrange("p (h d) -> p h d", h=16),
                                                                   axis=AX.X, op=ALU.add),
                  reads=[kt_.t()], writes=[zs.t(pg)])
        for b_ in kpg + ktmp2 + [qb, qbf]:
            ar.free(b_)
        zall = [zs.t(pg) for pg in range(NPAGES)]
        Es = ar.alloc("Es", [128, 16], F32)
        LKs = ar.alloc("LKs", [128, 16], F32)
        Ssum = ar.alloc("Ssum", [128, 16], F32)
        PTa = ar.alloc("PTa", [128, 16], F32)
        PTb = ar.alloc("PTb", [128, 16], F32)
        PT0 = ar.alloc("PT0", [128, 16], F32)
        wsb = ar.alloc("wsb", [128, 16], BF16)
        ZS = Tile("zs_all")
        for hh in range(H):
            P.add("act", lambda e, hh=hh: e.activation(out=Es.ap[:, :, hh], in_=zs.ap[:, :, hh], func=AF.Exp,
                                                       bias=bsb.ap[:, hh:hh + 1], scale=1.0),
                  reads=zall + [bsb.t()], writes=[Es.t(hh)])
        eall = [Es.t(hh) for hh in range(H)]
        P.add("act", lambda e: e.activation(out=LKs.ap, in_=Es.ap, func=AF.Ln, bias=1.0, scale=1.0), reads=eall, writes=[LKs.t()])
        LKf = LKs.ap.rearrange("p a b -> p (a b)")
        for n_ in range(4):
            P.add("pe", lambda e, n_=n_: e.matmul(bk(n_), lhsT=tri_f.ap, rhs=LKf[:, n_ * 512:(n_ + 1) * 512], start=True, stop=True),
                  reads=[LKs.t(), tri_f.t()], writes=[bank_t[n_]], name="s_tri")
            P.add("pe", lambda e, n_=n_: e.matmul(bk(4 + n_), lhsT=ones_f.ap, rhs=LKf[:, n_ * 512:(n_ + 1) * 512], start=True, stop=True),
                  reads=[LKs.t(), ones_f.t()], writes=[bank_t[4 + n_]], name="s_tot")
        Sf = Ssum.ap.rearrange("p a b -> p (a b)")
        PTaf = PTa.ap.rearrange("p a b -> p (a b)")
        PT0f = PT0.ap.rearrange("p a b -> p (a b)")
        for n_ in range(4):
            P.add("act", lambda e, n_=n_: e.copy(out=Sf[:, n_ * 512:(n_ + 1) * 512], in_=bk(n_)),
                  reads=[bank_t[n_]], writes=[Ssum.t(n_)])
            P.add("dve", lambda e, n_=n_: e.tensor_copy(out=PTaf[:, n_ * 512:(n_ + 1) * 512], in_=bk(4 + n_)),
                  reads=[bank_t[4 + n_]], writes=[PTa.t(n_)])
        P.add("pool", lambda e: e.tensor_copy(out=PT0f, in_=PTaf), reads=[PTa.t(n_) for n_ in range(4)], writes=[PT0.t()])
        cur, oth = PTa, PTb
        cur_t = [PTa.t(n_) for n_ in range(4)]
        sh = 1
        while sh < NPAGES:
            def scan(e, cur=cur, oth=oth, sh=sh):
                e.tensor_tensor(out=oth.ap[:, 0:NPAGES - sh, :], in0=cur.ap[:, 0:NPAGES - sh, :], in1=cur.ap[:, sh:NPAGES, :], op=ALU.add)
                return e.tensor_copy(out=oth.ap[:, NPAGES - sh:NPAGES, :], in_=cur.ap[:, NPAGES - sh:NPAGES, :])
            P.add("dve", scan, reads=cur_t, writes=[oth.t()])
            cur, oth = oth, cur
            cur_t = [cur.t()]
            sh *= 2
        curf = cur.ap.rearrange("p a b -> p (a b)")
        P.add("dve", lambda e: e.tensor_tensor(out=Sf, in0=Sf, in1=curf, op=ALU.add),
              reads=cur_t + [Ssum.t(n_) for n_ in range(4)], writes=[Ssum.t()])
        P.add("dve", lambda e: e.tensor_tensor(out=Sf, in0=Sf, in1=PT0f, op=ALU.subtract), reads=[Ssum.t(), PT0.t()], writes=[Ssum.t()])
        P.add("act", lambda e: e.activation(out=Sf, in_=Sf, func=AF.Exp, scale=-1.0), reads=[Ssum.t()], writes=[Ssum.t()])
        P.add("dve", lambda e: e.tensor_tensor(out=wsb.ap.rearrange("p a b -> p (a b)"), in0=Sf,
                                               in1=Es.ap.rearrange("p a b -> p (a b)"), op=ALU.mult),
              reads=[Ssum.t()] + eall, writes=[wsb.t()])
        for b_ in (Es, LKs, PTa, PTb, PT0):
            ar.free(b_)
        vpg = [ar.alloc(f"vpg{i}", [2048], BF16) for i in range(3)]
        for pg in range(NPAGES):
            slot = pg % 3

            P.dma("pool", lambda e, pg=pg, slot=slot: [e.indirect_dma_start(
                out=vpg[slot].ap, out_offset=None, in_=vrows,
                in_offset=bass.IndirectOffsetOnAxis(ap=pidx.ap[:, pg:pg + 1], axis=0))],
                f"vpg{slot}", 1, reads=[pidx.t()], writes=[vpg[slot].t()], name="vpage")

            def vacc(e, pg=pg, slot=slot):
                ins = None
                for n_ in range(4):
                    ins = e.matmul(banks[n_][0:16, :], lhsT=wsb.ap[:, pg, :], rhs=vpg[slot].ap[:, n_ * 512:(n_ + 1) * 512],
                                   start=(pg == 0), stop=(pg == NPAGES - 1))
                return ins
            P.add("pe", vacc, reads=[wsb.t(), vpg[slot].t()], writes=[bank_t[n_] for n_ in range(4)], name="s_av")
        oacc = ar.alloc("oacc", [2048], F32)
        dmask = ar.alloc("dmask", [2048], F32)
        P.add("pool", lambda e: e.memset(dmask.ap[0:16, :], 1.0), writes=[dmask.t()])
        P.add("pool", lambda e: e.affine_select(out=dmask.ap[0:16, :], in_=dmask.ap[0:16, :], pattern=[[1, 2048]],
                                                compare_op=ALU.is_ge, fill=0.0, base=0, channel_multiplier=-128),
              reads=[dmask.t()], writes=[dmask.t()])
        P.add("pool", lambda e: e.affine_select(out=dmask.ap[0:16, :], in_=dmask.ap[0:16, :], pattern=[[-1, 2048]],
                                                compare_op=ALU.is_ge, fill=0.0, base=127, channel_multiplier=128),
              reads=[dmask.t()], writes=[dmask.t()])
        for n_ in range(4):
            P.add("dve", lambda e, n_=n_: e.tensor_tensor(out=oacc.ap[0:16, n_ * 512:(n_ + 1) * 512], in0=banks[n_][0:16, :],
                                                          in1=dmask.ap[0:16, n_ * 512:(n_ + 1) * 512], op=ALU.mult),
                  reads=[bank_t[n_], dmask.t()], writes=[oacc.t(n_)])

        def odiag(e):
            ins = None
            for hh in range(H):
                ins = e.matmul(banks[SB][:, hh:hh + 1], lhsT=oacc.ap[0:16, hh * 128:(hh + 1) * 128], rhs=ones_f.ap[0:16, 0:1],
                               start=True, stop=True)
            return ins
        P.add("pe", odiag, reads=[oacc.t(n_) for n_ in range(4)] + [ones_f.t()], writes=[bank_t[SB]], name="odiag")
        P.add("dve", lambda e: e.tensor_copy(out=os_b.ap, in_=banks[SB][:, 0:16]), reads=[bank_t[SB]], writes=[os_b.t()])
        for b_ in vpg + [oacc, dmask, zs, Ssum, wsb]:
            ar.free(b_)

        hx = ar.alloc("hx", [16, 3], BF16)
        P.add("dve", lambda e: e.tensor_copy(out=hx.ap[:, :, 0:2], in_=hTo.ap[:, :, 1022:1024]),
              reads=[hTo.t((1, k)) for k in range(KC)], writes=[hx.t()])
        P.add("dve", lambda e: e.tensor_copy(out=hx.ap[:, :, 2], in_=hs.ap), reads=[hs.t(), hx.t()], writes=[hx.t()])
        ar.free(hTo)
        oconvT = ar.alloc("oconvT", [8, 1024], BF16, at=97)
        mov_hx = (lambda k: hx.ap[:, k, :], [hx.t()])
        ccs4 = ar.alloc("ccs4", [4, 1024], F32)
        U4 = ar.alloc("U4", [4, 1026], F32)
        for jq in range(2):
            wvc, wtc = wq.next(T["w_in"], 0, 2048, OFF_CC + jq * 512, 512)
            for j in range(4):
                b0, b1 = acc_rr.next(), acc_rr.next()
                fm_group(wvc, wtc, j, 16, mov_own + [mov_hx], [(bk(b0), b0), (bk(b1), b1), (banks[SB][:, 0:3], SB)], "cc")
                for g, b in ((0, b0), (1, b1)):
                    P.add("act", lambda e, g=g, b=b, j=j: e.copy(out=ccs4.ap[:, j, g * 512:(g + 1) * 512], in_=bk(b)),
                          reads=[bank_t[b]], writes=[ccs4.t((j, g))])
                P.add("dve", lambda e, j=j: e.tensor_copy(out=smc.ap[:, j, 0:3], in_=banks[SB][:, 0:3]), reads=[bank_t[SB]], writes=[smc.t(j)])
            wq.done()
            wvh, wth = wq.next(T["w_in"], 0, 2048, OFF_CH + jq * 512, 512)
            for j in range(4):
                jj = jq * 4 + j
                b0, b1 = acc_rr.next(), acc_rr.next()
                fm_group(wvh, wth, j, 16, mov_own + [mov_hx], [(bk(b0), b0), (bk(b1), b1), (banks[SB][:, 0:3], SB)], "ch")
                for g, b in ((0, b0), (1, b1)):
                    P.add("dve", lambda e, g=g, b=b, j=j: e.tensor_tensor(
                        out=U4.ap[:, j, 2 + g * 512:2 + (g + 1) * 512], in0=bk(b), in1=ccs4.ap[:, j, g * 512:(g + 1) * 512], op=ALU.mult),
                        reads=[bank_t[b], ccs4.t((j, g))], writes=[U4.t((j, g))])
                P.add("dve", lambda e, j=j: e.tensor_tensor(out=smc.ap[:, j, 4:7], in0=banks[SB][:, 0:3], in1=smc.ap[:, j, 0:3], op=ALU.mult),
                      reads=[bank_t[SB], smc.t(j)], writes=[smc.t(j)])
                P.add("dve", lambda e, j=j: e.tensor_scalar(out=U4.ap[:, j, 0:2], in0=smc.ap[:, j, 4:6], scalar1=flags.ap[:, 0:1],
                                                            scalar2=None, op0=ALU.mult),
                      reads=[smc.t(j), flags.t()], writes=[U4.t((j, 2))])
                ut = [U4.t((j, 0)), U4.t((j, 1)), U4.t((j, 2))]
                cvt = [ccs4.t((j, 0)), ccs4.t((j, 1))]
                P.add("dve", lambda e, j=j, jj=jj: e.tensor_scalar(out=ccs4.ap[:, j, :], in0=U4.ap[:, j, 0:1024], scalar1=cw.ap[:, jj, 0:1],
                                                                   scalar2=None, op0=ALU.mult),
                      reads=ut + [cw.t()], writes=cvt)
                for tap in (1, 2):
                    P.add("dve", lambda e, j=j, jj=jj, tap=tap: e.scalar_tensor_tensor(
                        out=ccs4.ap[:, j, :], in0=U4.ap[:, j, tap:tap + 1024], scalar=cw.ap[:, jj, tap:tap + 1], in1=ccs4.ap[:, j, :],
                        op0=ALU.mult, op1=ALU.add),
                        reads=ut + [cw.t()] + cvt, writes=cvt)
                P.add("dve", lambda e, j=j, jj=jj: e.tensor_copy(out=cpst.ap[:, jj, :], in_=U4.ap[:, j, 1024:1026]),
                      reads=ut, writes=[cpst.t(jj)])
                P.add("dve", lambda e, j=j, jj=jj: e.tensor_tensor(out=smc.ap[:, j, 8:10], in0=stc.ap[:, jj, :], in1=cw.ap[:, jj, 0:2], op=ALU.mult),
                      reads=[stc.t(), cw.t(), smc.t(j)], writes=[smc.t(j)])
                P.add("dve", lambda e, j=j: e.tensor_tensor(out=smc.ap[:, j, 10:11], in0=smc.ap[:, j, 8:9], in1=smc.ap[:, j, 9:10], op=ALU.add),
                      reads=[smc.t(j)], writes=[smc.t(j)])
                P.add("dve", lambda e, j=j, jj=jj: e.scalar_tensor_tensor(out=smc.ap[:, j, 11:12], in0=smc.ap[:, j, 6:7], scalar=cw.ap[:, jj, 2:3],
                                                                          in1=smc.ap[:, j, 10:11], op0=ALU.mult, op1=ALU.add),
                      reads=[smc.t(j), cw.t()], writes=[smc.t(j)])
                P.add("dve", lambda e, jj=jj: e.tensor_copy(out=csst.ap[:, jj, 0:1], in_=stc.ap[:, jj, 1:2]),
                      reads=[stc.t()], writes=[csst.t((jj, 0))])
                P.add("dve", lambda e, j=j, jj=jj: e.tensor_copy(out=csst.ap[:, jj, 1:2], in_=smc.ap[:, j, 6:7]),
                      reads=[smc.t(j)], writes=[csst.t((jj, 1))])
            wq.done()
            wvb, wtb = wq.next(T["w_in"], 0, 2048, OFF_CB + jq * 512, 512)
            for j in range(4):
                jj = jq * 4 + j
                b0, b1 = acc_rr.next(), acc_rr.next()
                fm_group(wvb, wtb, j, 16, mov_own + [mov_hs], [(bk(b0), b0), (bk(b1), b1), (banks[SB][:, 0:1], SB)], "cb")
                for g, b in ((0, b0), (1, b1)):
                    P.add("dve", lambda e, g=g, b=b, j=j, jj=jj: e.tensor_tensor(
                        out=oconvT.ap[:, jj, g * 512:(g + 1) * 512], in0=bk(b), in1=ccs4.ap[:, j, g * 512:(g + 1) * 512], op=ALU.mult),
                        reads=[bank_t[b], ccs4.t((j, g))], writes=[oconvT.t((g, jj))])
                P.add("dve", lambda e, j=j, jj=jj: e.tensor_tensor(out=ocs_b.ap[:, jj:jj + 1], in0=banks[SB][:, 0:1], in1=smc.ap[:, j, 11:12], op=ALU.mult),
                      reads=[bank_t[SB], smc.t(j)], writes=[ocs_b.t(jj)])
            wq.done()
        P.dma("sp", lambda e: [e.dma_start(out=fm_vec(T["conv_p"][jt, :]), in_=cpst.ap[:, :, jt], **NCD) for jt in range(2)]
              + [e.dma_start(out=fm_vec(T["conv_s"][jt, :]), in_=csst.ap[:, :, jt], **NCD) for jt in range(2)],
              "convout", 4, reads=[cpst.t(jj) for jj in range(8)] + [csst.t((jj, i)) for jj in range(8) for i in range(2)], final=True)
        for b_ in (ccs4, U4, hx):
            ar.free(b_)

        mergedT = ar.alloc("mergedT", [16, 1024], BF16, at=0)
        sg4 = ar.alloc("sg4", [4, 1024], F32, at=32)
        m14 = ar.alloc("m14", [4, 1024], F32, at=48)
        mov_o = [(lambda k, g=g: oT.ap[:, k, g * 512:(g + 1) * 512], [oT.t((g, k)) for k in range(H)]) for g in range(2)]
        mov_oc = [(lambda k, g=g: oconvT.ap[:, k, g * 512:(g + 1) * 512], [oconvT.t((g, k)) for k in range(8)]) for g in range(2)]
        mov_os = (lambda k: os_b.ap[:, k:k + 1], [os_b.t()])
        mov_ocs = (lambda k: ocs_b.ap[:, k:k + 1], [ocs_b.t(k) for k in range(8)])

        def gate_tile(w, off, cq):
            wv_, wt_ = wq.next(w, 0, 2048, off + cq * 512, 512)
            for j in range(4):
                b0, b1 = acc_rr.next(), acc_rr.next()
                fm_group(wv_, wt_, j, 16, mov_own + [mov_hs], [(bk(b0), b0), (bk(b1), b1), (banks[SB][:, 0:1], SB)], "gate")
                for g, b in ((0, b0), (1, b1)):
                    P.add("act", lambda e, g=g, b=b, j=j: e.activation(out=sg4.ap[:, j, g * 512:(g + 1) * 512], in_=bk(b), func=AF.Sigmoid),
                          reads=[bank_t[b]], writes=[sg4.t((j, g))])
                P.add("act", lambda e, j=j: e.activation(out=smc.ap[:, j, 0:1], in_=banks[SB][:, 0:1], func=AF.Sigmoid),
                      reads=[bank_t[SB]], writes=[smc.t(j)])
            wq.done()

        for cq in range(4):
            gate_tile(T["w_in"], OFF_GA, cq)
            wva, wta = wq.next(T["w_attn_out"], 0, 2048, cq * 512, 512)
            for j in range(4):
                b0, b1 = acc_rr.next(), acc_rr.next()
                fm_group(wva, wta, j, 16, mov_o + [mov_os], [(bk(b0), b0), (bk(b1), b1), (banks[SB][:, 0:1], SB)], "A")
                for g, b in ((0, b0), (1, b1)):
                    P.add("dve", lambda e, g=g, b=b, j=j: e.tensor_tensor(out=m14.ap[:, j, g * 512:(g + 1) * 512], in0=bk(b),
                                                                          in1=sg4.ap[:, j, g * 512:(g + 1) * 512], op=ALU.mult),
                          reads=[bank_t[b], sg4.t((j, g))], writes=[m14.t((j, g))])
                P.add("dve", lambda e, j=j: e.tensor_tensor(out=smc.ap[:, j, 1:2], in0=banks[SB][:, 0:1], in1=smc.ap[:, j, 0:1], op=ALU.mult),
                      reads=[bank_t[SB], smc.t(j)], writes=[smc.t(j)])
            wq.done()
            gate_tile(T["w_in"], OFF_GB, cq)
            wvc, wtc = wq.next(T["w_conv_out"], 0, 1024, cq * 512, 512)
            for j in range(4):
                c = cq * 4 + j
                b0, b1 = acc_rr.next(), acc_rr.next()
                fm_group(wvc, wtc, j, 8, mov_oc + [mov_ocs], [(bk(b0), b0), (bk(b1), b1), (banks[SB][:, 0:1], SB)], "C")
                for g, b in ((0, b0), (1, b1)):
                    P.add("dve", lambda e, g=g, b=b, j=j: e.tensor_tensor(out=sg4.ap[:, j, g * 512:(g + 1) * 512], in0=bk(b),
                                                                          in1=sg4.ap[:, j, g * 512:(g + 1) * 512], op=ALU.mult),
                          reads=[bank_t[b], sg4.t((j, g))], writes=[sg4.t((j, g))])
                    P.add("dve", lambda e, g=g, j=j, c=c: e.tensor_tensor(out=mergedT.ap[:, c, g * 512:(g + 1) * 512],
                                                                          in0=sg4.ap[:, j, g * 512:(g + 1) * 512],
                                                                          in1=m14.ap[:, j, g * 512:(g + 1) * 512], op=ALU.add),
                          reads=[sg4.t((j, g)), m14.t((j, g))], writes=[mergedT.t((g, c))])
                P.add("dve", lambda e, j=j: e.tensor_tensor(out=smc.ap[:, j, 2:3], in0=banks[SB][:, 0:1], in1=smc.ap[:, j, 0:1], op=ALU.mult),
                      reads=[bank_t[SB], smc.t(j)], writes=[smc.t(j)])
                P.add("dve", lambda e, j=j, c=c: e.tensor_tensor(out=mgs_b.ap[:, c:c + 1], in0=smc.ap[:, j, 1:2], in1=smc.ap[:, j, 2:3], op=ALU.add),
                      reads=[smc.t(j)], writes=[mgs_b.t(c)])
            wq.done()
        for b_ in (sg4, m14, oT, oconvT):
            ar.free(b_)

        ar.free(hT)
        xT = ar.alloc("xT", [16, 1024], F32, at=65)
        xin5 = ar.alloc("xin5", [4, 2048], F32, at=129)
        for g in range(2):
            load_x(T["x_own"], g, xin5)
            transpose_group(xin5, lambda k, g=g: xT.ap[:, k, g * 512:(g + 1) * 512], lambda k, g=g: xT.t((g, k)))
        ar.free(xin5)
        mov_m = [(lambda k, g=g: mergedT.ap[:, k, g * 512:(g + 1) * 512], [mergedT.t((g, k)) for k in range(KC)]) for g in range(2)]
        mov_ms = (lambda k: mgs_b.ap[:, k:k + 1], [mgs_b.t(k) for k in range(KC)])

        def resid_update(b, c, g, gate_fn):
            P.add("dve", lambda e: e.scalar_tensor_tensor(out=xT.ap[:, c, g * 512:(g + 1) * 512], in0=bk(b), scalar=gate_fn(c, 0),
                                                          in1=xT.ap[:, c, g * 512:(g + 1) * 512], op0=ALU.mult, op1=ALU.add),
                  reads=[bank_t[b], xT.t((g, c)), MODT], writes=[xT.t((g, c))])

        def resid_update_s(col, c, gate_fn):
            P.add("dve", lambda e: e.scalar_tensor_tensor(out=xs.ap[:, c:c + 1], in0=banks[SB][:, col:col + 1], scalar=gate_fn(c, 1),
                                                          in1=xs.ap[:, c:c + 1], op0=ALU.mult, op1=ALU.add),
                  reads=[bank_t[SB], xs.t(), MODT], writes=[xs.t()])
        for cq in range(4):
            wvo, wto = wq.next(T["w_out"], 0, 2048, cq * 512, 512)
            for j in range(4):
                c = cq * 4 + j
                b0, b1 = acc_rr.next(), acc_rr.next()
                fm_group(wvo, wto, j, 16, mov_m + [mov_ms], [(bk(b0), b0), (bk(b1), b1), (banks[SB][:, 0:1], SB)], "mix")
                resid_update(b0, c, 0, GATE1)
                resid_update(b1, c, 1, GATE1)
                resid_update_s(0, c, GATE1)
            wq.done()
        ar.free(mergedT)

        hT2 = ar.alloc("hT2", [16, 1024], BF16, at=129)
        sqb2 = [ar.alloc(f"sqb2{i}", [512], BF16) for i in range(2)]
        rstd2 = ar.alloc("rstd2", [512], F32)
        ntmp2 = [ar.alloc(f"ntmp2{i}", [512], F32) for i in range(2)]
        mov_own2 = [(lambda k, g=g: hT2.ap[:, k, g * 512:(g + 1) * 512], [hT2.t((g, k)) for k in range(KC)]) for g in range(2)]
        for g in range(2):
            group_norm_stats(lambda k, g=g: xT.ap[:, k, g * 512:(g + 1) * 512], lambda k, g=g: [xT.t((g, k))], sqb2, rstd2, f"n2_{g}")
            norm_apply(lambda k, g=g: xT.ap[:, k, g * 512:(g + 1) * 512], lambda k, g=g: xT.t((g, k)), hT2, g,
                       lambda k: A2.ap[:, k, 0:1], lambda k: B2(k, 0), rstd2, ntmp2)
        sample_norm(A2.ap[:, :, 1], modT.ap[:, 48:64, 1], hs, "sn2")
        for b_ in sqb2 + ntmp2 + [rstd2]:
            ar.free(b_)

        gT = ar.alloc("gT", [22, 1024], BF16, at=0)
        sl4 = ar.alloc("sl4", [4, 1024], F32, at=44)
        for half in range(2):
            for fq in range(6):
                ncol = 512 if fq < 5 else 256
                nch = ncol // 128
                c0 = half * 2816 + fq * 512
                wva_, wta_ = wq.next(T["w_ffn_in"], 0, 2048, c0, ncol)
                for j in range(nch):
                    a0, a1 = acc_rr.next(), acc_rr.next()
                    fm_group(wva_, wta_, j, 16, mov_own2 + [mov_hs], [(bk(a0), a0), (bk(a1), a1), (banks[SB][:, 0:1], SB)], "ffa")
                    for g, b in ((0, a0), (1, a1)):
                        P.add("act", lambda e, g=g, b=b, j=j: e.activation(out=sl4.ap[:, j, g * 512:(g + 1) * 512], in_=bk(b), func=AF.Silu),
                              reads=[bank_t[b]], writes=[sl4.t((j, g))])
                    P.add("act", lambda e, j=j: e.activation(out=smc.ap[:, j, 0:1], in_=banks[SB][:, 0:1], func=AF.Silu),
                          reads=[bank_t[SB]], writes=[smc.t(j)])
                wq.done()
                wvb_, wtb_ = wq.next(T["w_ffn_in"], 0, 2048, DFF + c0, ncol)
                for j in range(nch):
                    jl = fq * 4 + j
                    jj = half * 22 + jl
                    b0, b1 = acc_rr.next(), acc_rr.next()
                    fm_group(wvb_, wtb_, j, 16, mov_own2 + [mov_hs], [(bk(b0), b0), (bk(b1), b1), (banks[SB][:, 0:1], SB)], "ffb")
                    for g, b in ((0, b0), (1, b1)):
                        P.add("dve", lambda e, g=g, b=b, j=j, jl=jl: e.tensor_tensor(out=gT.ap[:, jl, g * 512:(g + 1) * 512], in0=bk(b),
                                                                                     in1=sl4.ap[:, j, g * 512:(g + 1) * 512], op=ALU.mult),
                              reads=[bank_t[b], sl4.t((j, g))], writes=[gT.t((g, jl))])
                    P.add("dve", lambda e, j=j, jj=jj: e.tensor_tensor(out=gs_b.ap[:, jj:jj + 1], in0=banks[SB][:, 0:1], in1=smc.ap[:, j, 0:1], op=ALU.mult),
                          reads=[bank_t[SB], smc.t(j)], writes=[gs_b.t(jj)])
                wq.done()
            mov_g = [(lambda k, g=g: gT.ap[:, k, g * 512:(g + 1) * 512], [gT.t((g, k)) for k in range(22)]) for g in range(2)]
            mov_gs = (lambda k, half=half: gs_b.ap[:, half * 22 + k:half * 22 + k + 1], [gs_b.t(half * 22 + k) for k in range(22)])
            for c2 in range(8):
                wvf, wtf = wq.next(T["w_ffn_out"], half * 2816, 2816, c2 * 256, 256)
                for j in range(2):
                    c = c2 * 2 + j
                    b0, b1 = acc_rr.next(), acc_rr.next()
                    fm_group(wvf, wtf, j, 22, mov_g + [mov_gs], [(bk(b0), b0), (bk(b1), b1), (banks[SB][:, 0:1], SB)], "ffo")
                    resid_update(b0, c, 0, GATE2)
                    resid_update(b1, c, 1, GATE2)
                    resid_update_s(0, c, GATE2)
                wq.done()
        for b_ in (sl4, gT):
            ar.free(b_)
        ar.free(hT2)

        yst = ar.alloc("yst", [4, 2048], F32)
        sqb3 = [ar.alloc(f"sqb3{i}", [512], BF16) for i in range(2)]
        rstd3 = ar.alloc("rstd3", [512], F32)
        ntmp3 = [ar.alloc(f"ntmp3{i}", [512], F32) for i in range(2)]
        yT = [ar.alloc(f"yT{i}", [512], F32) for i in range(2)]
        for g in range(2):
            group_norm_stats(lambda k, g=g: xT.ap[:, k, g * 512:(g + 1) * 512], lambda k, g=g: [xT.t((g, k))], sqb3, rstd3, f"n3_{g}")
            for c in range(KC):
                tb_, y_ = ntmp3[c % 2], yT[c % 2]
                P.add("dve", lambda e, c=c, g=g, tb_=tb_: e.tensor_tensor(out=tb_.ap, in0=xT.ap[:, c, g * 512:(g + 1) * 512], in1=rstd3.ap, op=ALU.mult),
                      reads=[xT.t((g, c)), rstd3.t()], writes=[tb_.t()])
                P.add("act", lambda e, c=c, tb_=tb_, y_=y_: e.activation(out=y_.ap, in_=tb_.ap, func=AF.Identity, scale=gfin.ap[:, c:c + 1]),
                      reads=[tb_.t(), gfin.t()], writes=[y_.t()])
                b = acc_rr.next()

                def ytr(e, y_=y_, b=b):
                    ins = None
                    for j in range(4):
                        ins = e.transpose(out=banks[b][:, j * 128:(j + 1) * 128], in_=y_.ap[:, j * 128:(j + 1) * 128], identity=ident_f.ap)
                    return ins
                P.add("pe", ytr, reads=[y_.t(), ident_f.t()], writes=[bank_t[b]], name="ytr")
                P.add("dve", lambda e, c=c, b=b: e.tensor_copy(out=yst.ap[:, :, c * 128:(c + 1) * 128],
                                                                in_=banks[b][:, :].rearrange("p (a f) -> p a f", a=4)),
                      reads=[bank_t[b]], writes=[yst.t(c)])
            ydst = T["y_own"][g * 512:(g + 1) * 512, :].rearrange("(b p) f -> p b f", p=128)
            P.dma("sp", lambda e, ydst=ydst: [e.dma_start(out=ydst, in_=yst.ap)], "yst", 1,
                  reads=[yst.t(c) for c in range(KC)], final=True, name="yout")
        ys = ar.alloc("ys", [16], F32)
        sample_norm(gfin.ap, None, ys, "sn3")
        P.dma("sp", lambda e: [e.dma_start(out=fm_vec(T["y_s"]), in_=ys.ap, **NCD)], "ys", 1, reads=[ys.t()], final=True)

        counts = P.emit(st)
        build_core_program.info = dict(signals=counts, peak=ar.peak, nops=len(P.ops))
        return wq.rec


IN_SPECS = [
    ("x_own", [1024, 2048], F32), ("x_oth", [1024, 2048], F32), ("x_s", [2048], F32),
    ("c_p", [2048], F32), ("c_s", [2048], F32), ("flags", [128, 2], F32),
    ("cache_k", [NPOOL, 128, 2048], F32), ("cache_v", [NPOOL, 128, 2048], F32),
    ("page_tbl", [1, NPAGES], I32), ("state_conv", [2, 1024], F32),
    ("g_mix", [2048], F32), ("g_ffn", [2048], F32), ("g_final", [2048], F32),
    ("w_mod", [2048, 12288], F32), ("b_mod", [12288], F32), ("w_in", [2048, D_IN], F32),
    ("b_sb", [16], F32), ("conv_w", [3, 1024], F32), ("w_attn_out", [2048, 2048], F32),
    ("w_conv_out", [1024, 2048], F32), ("w_out", [2048, 2048], F32),
    ("w_ffn_in", [2048, 2 * DFF], F32), ("w_ffn_out", [DFF, 2048], F32),
]
OUT_SPECS = [
    ("y_own", [1024, 2048]), ("y_s", [2048]), ("k_own", [1024, 2048]), ("v_own", [1024, 2048]),
    ("conv_p", [2, 1024]), ("k_s", [2048]), ("v_s", [2048]), ("conv_s", [2, 1024]),
]


def build_nc():
    plan = None
    for it in range(2):
        nc = bass.Bass("TRN2", target_bir_lowering=False)
        T = {}
        for (n, shp, dt) in IN_SPECS:
            T[n] = nc.dram_tensor(n, shp, dt, kind="ExternalInput").ap()
        for (n, shp) in OUT_SPECS:
            T[n] = nc.dram_tensor(n, shp, F32, kind="ExternalOutput").ap()
        T["q_scr"] = nc.dram_tensor("q_scr", [2048], F32, kind="Internal").ap()
        plan = build_core_program(nc, T, plan)
    return nc


def make_in_maps(inp):
    f = lambda a: np.ascontiguousarray(np.asarray(a, dtype=np.float32))
    shared = {
        "cache_k": f(inp["cache_k"][0]).reshape(NPOOL, 128, 2048), "cache_v": f(inp["cache_v"][0]).reshape(NPOOL, 128, 2048),
        "g_mix": f(inp["g_mix"][0]), "g_ffn": f(inp["g_ffn"][0]), "g_final": f(inp["g_final"]),
        "w_mod": f(inp["w_mod"][0]), "b_mod": f(inp["b_mod"][0]), "w_in": f(inp["w_in"][0]), "b_sb": f(inp["b_sb"][0]),
        "conv_w": f(inp["conv_w"][0]), "w_attn_out": f(inp["w_attn_out"][0]), "w_conv_out": f(inp["w_conv_out"][0]),
        "w_out": f(inp["w_out"][0]), "w_ffn_in": f(inp["w_ffn_in"][0]), "w_ffn_out": f(inp["w_ffn_out"][0]),
    }
    maps = []
    for c in range(8):
        b, par = c // 2, c % 2
        m = dict(shared)
        xp = np.asarray(inp["x_prompt"], dtype=np.float32)
        m["x_own"] = np.ascontiguousarray(xp[b, par * 1024:(par + 1) * 1024])
        m["x_oth"] = np.ascontiguousarray(xp[b, (1 - par) * 1024:(2 - par) * 1024])
        m["x_s"] = f(inp["x_sample"][c, 0])
        m["c_p"] = f(inp["c_prompt"][b])
        m["c_s"] = f(inp["c_sample"][c])
        fl = np.zeros((128, 2), np.float32)
        fl[:, 0] = float(par)
        fl[:, 1] = 0.0 if par == 1 else NEG
        m["flags"] = fl
        m["page_tbl"] = np.ascontiguousarray(np.asarray(inp["page_table"], dtype=np.int32)[c:c + 1])
        m["state_conv"] = f(inp["state_conv"][0, c])
        maps.append(m)
    return maps


def assemble(results):
    y_prompt = np.zeros((4, 2048, 2048), np.float32)
    k_prompt = np.zeros((1, 4, 2048, 16, 128), np.float32)
    v_prompt = np.zeros((1, 4, 2048, 16, 128), np.float32)
    conv_prompt = np.zeros((1, 4, 2, 1024), np.float32)
    y_sample = np.zeros((8, 1, 2048), np.float32)
    k_sample = np.zeros((1, 8, 1, 16, 128), np.float32)
    v_sample = np.zeros((1, 8, 1, 16, 128), np.float32)
    conv_sample = np.zeros((1, 8, 2, 1024), np.float32)
    for c in range(8):
        r = results[c]
        b, par = c // 2, c % 2
        sl = slice(par * 1024, (par + 1) * 1024)
        y_prompt[b, sl] = r["y_own"]
        k_prompt[0, b, sl] = r["k_own"].reshape(1024, 16, 128)
        v_prompt[0, b, sl] = r["v_own"].reshape(1024, 16, 128)
        if par == 1:
            conv_prompt[0, b] = r["conv_p"]
        y_sample[c, 0] = r["y_s"]
        k_sample[0, c, 0] = r["k_s"].reshape(16, 128)
        v_sample[0, c, 0] = r["v_s"].reshape(16, 128)
        conv_sample[0, c] = r["conv_s"]
    return (y_prompt, y_sample, k_prompt, v_prompt, conv_prompt, k_sample, v_sample, conv_sample)


_NC_CACHE = {}


def kernel(**inputs):
    if "nc" not in _NC_CACHE:
        _NC_CACHE["nc"] = build_nc()
    nc = _NC_CACHE["nc"]
    maps = make_in_maps(inputs)
    res = run_bass_kernel_spmd(nc, maps, core_ids=list(range(8)))
    return assemble(res.results)
```
